# Optimizing a Trainium2 kernel written in Bass

```python
import math
import jax, jax.numpy as jnp
from jax import lax
import numpy as np

D_MODEL = 1024
BATCH = 8
SEQ = 8192
DEPTH = 2
DEC_BATCH = 32
DEC_SEQ = 2048
PAST_LEN = 128

HEAD_DIM = 64
N_HEADS = D_MODEL // HEAD_DIM
NA_HEADS = N_HEADS // 4
DIFF_HEADS = N_HEADS // 4
DIL_HEADS = N_HEADS - NA_HEADS - DIFF_HEADS
NA_WIDTH = NA_HEADS * HEAD_DIM
DIFF_WIDTH = DIFF_HEADS * HEAD_DIM
DIL_WIDTH = DIL_HEADS * HEAD_DIM
MIX_WIDTH = NA_WIDTH + DIFF_WIDTH + DIL_WIDTH
IN_WIDTH = 3 * MIX_WIDTH
GRID_W = 64
NA_WIN_ROWS = 8
NA_WIN_COLS = 16
DIFF_QK_DIM = HEAD_DIM // 2
DIFF_BLOCK = 128
DIL_PATTERNS = ((128, 1), (512, 4), (2048, 16))
FFN_HIDDEN = -(-8 * D_MODEL // (3 * 256)) * 256
ROPE_THETA = 10000.0
LN_EPS = 1e-5
DEEPNORM_ALPHA = (2 * DEPTH) ** 0.25
DEEPNORM_BETA = (8 * DEPTH) ** -0.25
NEG_INF = -1e30

kernel_name = "hybrid_natten_diff_dilated_encoder"


def layer_norm(x, g, b):
    xf = x.astype(jnp.float32)
    mu = jnp.mean(xf, axis=-1, keepdims=True)
    var = jnp.mean(jnp.square(xf - mu), axis=-1, keepdims=True)
    return ((xf - mu) * lax.rsqrt(var + LN_EPS) * g + b).astype(x.dtype)


def apply_rope(x):
    S, dim = x.shape[1], x.shape[-1]
    half = dim // 2
    inv_freq = ROPE_THETA ** (-jnp.arange(half, dtype=jnp.float32) / half)
    ang = jnp.arange(S, dtype=jnp.float32)[:, None] * inv_freq[None, :]
    shape = (S,) + (1,) * (x.ndim - 3) + (half,)
    cos = jnp.cos(ang).reshape(shape)
    sin = jnp.sin(ang).reshape(shape)
    xf = x.astype(jnp.float32)
    x1, x2 = xf[..., :half], xf[..., half:]
    return jnp.concatenate([x1 * cos - x2 * sin, x1 * sin + x2 * cos], axis=-1).astype(x.dtype)


def neighbourhood_attention(q, k, v, rpb):
    B, S, _ = q.shape
    H, dh = NA_HEADS, HEAD_DIM
    R = S // GRID_W
    wr = min(NA_WIN_ROWS, R)
    wc = NA_WIN_COLS
    q = q.reshape(B, R, GRID_W, H, dh)
    k = k.reshape(B, R, GRID_W, H, dh)
    v = v.reshape(B, R, GRID_W, H, dh)
    r_idx = jnp.arange(R)
    rows = jnp.clip(r_idx - wr // 2, 0, R - wr)[:, None] + jnp.arange(wr)[None, :]
    k_rows = k[:, rows]
    v_rows = v[:, rows]
    c_idx = jnp.arange(GRID_W)
    c_start = jnp.clip(c_idx - wc // 2, 0, GRID_W - wc)
    col_in = (c_idx[None, :] >= c_start[:, None]) & (c_idx[None, :] < c_start[:, None] + wc)
    dr = rows - r_idx[:, None] + (NA_WIN_ROWS - 1)
    dc = jnp.clip(c_idx[None, :] - c_idx[:, None] + (NA_WIN_COLS - 1), 0, 2 * NA_WIN_COLS - 2)
    bias = jnp.take(rpb[:, dr], dc, axis=-1)
    bias = bias.transpose(0, 1, 3, 2, 4).astype(jnp.float32)
    s = jnp.einsum('brchd,brjkhd->bhrcjk', q, k_rows).astype(jnp.float32) * (dh ** -0.5) + bias[None]
    s = jnp.where(col_in[:, None, :], s, NEG_INF)
    p = jax.nn.softmax(s.reshape(B, H, R, GRID_W, wr * GRID_W), axis=-1).reshape(s.shape)
    o = jnp.einsum('bhrcjk,brjkhd->brchd', p.astype(v.dtype), v_rows)
    return o.reshape(B, S, H * dh)


def differential_attention(q, k, v, lam_vecs, subln_g, lambda_init):
    B, S, _ = q.shape
    H, dq, dv = DIFF_HEADS, DIFF_QK_DIM, HEAD_DIM
    out_dtype = v.dtype
    qr = apply_rope(q.reshape(B, S, H, 2, dq)).transpose(0, 2, 3, 1, 4)
    kr = apply_rope(k.reshape(B, S, H, 2, dq)).transpose(0, 2, 3, 1, 4)
    vr = v.reshape(B, S, H, dv).transpose(0, 2, 1, 3)
    lf = lam_vecs.astype(jnp.float32)
    lam = jnp.exp(jnp.sum(lf[0] * lf[1])) - jnp.exp(jnp.sum(lf[2] * lf[3])) + lambda_init
    nb = S // DIFF_BLOCK
    q_blocks = qr.reshape(B, H, 2, nb, DIFF_BLOCK, dq).transpose(3, 0, 1, 2, 4, 5)
    scale = dq ** -0.5

    def block(qb):
        s = jnp.einsum('bhiqd,bhikd->bhiqk', qb, kr).astype(jnp.float32) * scale
        p = jax.nn.softmax(s, axis=-1)
        a = p[:, :, 0] - lam * p[:, :, 1]
        return jnp.einsum('bhqk,bhkd->bhqd', a.astype(vr.dtype), vr)

    o = lax.map(block, q_blocks)
    o = o.transpose(1, 0, 3, 2, 4).reshape(B, S, H, dv).astype(jnp.float32)
    o = o * lax.rsqrt(jnp.mean(o * o, axis=-1, keepdims=True) + LN_EPS) * subln_g * (1.0 - lambda_init)
    return o.astype(out_dtype).reshape(B, S, H * dv)


def banded_window_attention(q, k, v, half_width):
    lead = q.shape[:-2]
    L, dh = q.shape[-2], q.shape[-1]
    hw = half_width
    nb = -(-L // hw)
    Lp = nb * hw
    pad = [(0, 0)] * len(lead)
    qp = jnp.pad(q, pad + [(0, Lp - L), (0, 0)]).reshape(*lead, nb, hw, dh)
    kp = jnp.pad(k, pad + [(hw, Lp - L + hw), (0, 0)])
    vp = jnp.pad(v, pad + [(hw, Lp - L + hw), (0, 0)])

    def neighbour_blocks(a):
        return jnp.concatenate([a[..., j * hw:j * hw + Lp, :].reshape(*lead, nb, hw, dh) for j in range(3)], axis=-2)

    kb = neighbour_blocks(kp)
    vb = neighbour_blocks(vp)
    qpos = jnp.arange(Lp).reshape(nb, hw)
    kpos = jnp.arange(nb)[:, None] * hw - hw + jnp.arange(3 * hw)[None, :]
    mask = (jnp.abs(qpos[:, :, None] - kpos[:, None, :]) <= hw) & ((kpos >= 0) & (kpos < L))[:, None, :]
    s = jnp.einsum('...nqd,...nkd->...nqk', qp, kb).astype(jnp.float32) * (dh ** -0.5)
    s = jnp.where(mask, s, NEG_INF)
    m = jnp.max(s, axis=-1)
    p = jnp.exp(s - m[..., None])
    l = jnp.sum(p, axis=-1)
    o = jnp.einsum('...nqk,...nkd->...nqd', p.astype(v.dtype), vb).astype(jnp.float32)
    return (o.reshape(*lead, Lp, dh)[..., :L, :], m.reshape(*lead, Lp)[..., :L], l.reshape(*lead, Lp)[..., :L])


def to_streams(a, dilation):
    B, S, H, dh = a.shape
    return a.reshape(B, S // dilation, dilation, H, dh).transpose(0, 2, 3, 1, 4)


def dilated_attention(q, k, v):
    B, S, _ = q.shape
    H, dh = DIL_HEADS, HEAD_DIM
    out_dtype = v.dtype
    qr = apply_rope(q.reshape(B, S, H, dh))
    kr = apply_rope(k.reshape(B, S, H, dh))
    vr = v.reshape(B, S, H, dh)
    outs, maxes, dens = [], [], []
    for window, dilation in DIL_PATTERNS:
        o, m, l = banded_window_attention(to_streams(qr, dilation), to_streams(kr, dilation),
                                          to_streams(vr, dilation), window // (2 * dilation))
        outs.append(o.transpose(0, 3, 1, 2, 4).reshape(B, S, H, dh))
        maxes.append(m.transpose(0, 3, 1, 2).reshape(B, S, H))
        dens.append(l.transpose(0, 3, 1, 2).reshape(B, S, H))
    m_all = jnp.max(jnp.stack(maxes, axis=0), axis=0)
    wts = [jnp.exp(m - m_all) for m in maxes]
    num = sum(w[..., None] * o for w, o in zip(wts, outs))
    den = sum(w * l for w, l in zip(wts, dens))
    return (num / den[..., None]).astype(out_dtype).reshape(B, S, H * dh)


def encoder_trunk(x, c, w_ada, b_ada, w_in, na_rpb, diff_lambda, diff_subln_g, w_out,
                  ln1_g, ln1_b, w_gu, w_down, ln2_g, ln2_b):
    splits = [NA_WIDTH, 2 * NA_WIDTH, 3 * NA_WIDTH,
              3 * NA_WIDTH + DIFF_WIDTH, 3 * NA_WIDTH + 2 * DIFF_WIDTH, 3 * NA_WIDTH + 3 * DIFF_WIDTH,
              3 * NA_WIDTH + 3 * DIFF_WIDTH + DIL_WIDTH, 3 * NA_WIDTH + 3 * DIFF_WIDTH + 2 * DIL_WIDTH]
    for layer in range(DEPTH):
        lambda_init = 0.8 - 0.6 * math.exp(-0.3 * layer)
        mod = jax.nn.silu(c) @ w_ada[layer] + b_ada[layer]
        sh1, sc1, g1, sh2, sc2, g2 = [m[:, None, :] for m in jnp.split(mod, 6, axis=-1)]
        h = x * (1.0 + sc1) + sh1
        proj = h @ w_in[layer]
        qa, ka, va, qb, kb, vb, qc, kc, vc = jnp.split(proj, splits, axis=-1)
        oa = neighbourhood_attention(qa, ka, va, na_rpb[layer])
        ob = differential_attention(qb, kb, vb, diff_lambda[layer], diff_subln_g[layer], lambda_init)
        oc = dilated_attention(qc, kc, vc)
        mix = jnp.concatenate([oa, ob, oc], axis=-1) @ w_out[layer]
        x = layer_norm(DEEPNORM_ALPHA * x + g1 * mix, ln1_g[layer], ln1_b[layer])
        h = x * (1.0 + sc2) + sh2
        gate, up = jnp.split(h @ w_gu[layer], 2, axis=-1)
        ffn = (jax.nn.silu(gate) * up) @ w_down[layer]
        x = layer_norm(DEEPNORM_ALPHA * x + g2 * ffn, ln2_g[layer], ln2_b[layer])
    return x


def setup_inputs(seed: int = 0) -> dict:
    key = jax.random.key(seed)
    ks = jax.random.split(key, 18)
    f32 = jnp.float32
    nrm = lambda k, shape: jax.random.normal(k, shape, dtype=f32)
    return {
        'x_prompt': nrm(ks[0], (BATCH, SEQ, D_MODEL)),
        'x_sample': nrm(ks[1], (DEC_BATCH, DEC_SEQ, D_MODEL)),
        'c_prompt': nrm(ks[2], (BATCH, D_MODEL)),
        'c_sample': nrm(ks[3], (DEC_BATCH, D_MODEL)),
        'w_ada': nrm(ks[4], (DEPTH, D_MODEL, 6 * D_MODEL)) * D_MODEL ** -0.5,
        'b_ada': 0.01 * nrm(ks[5], (DEPTH, 6 * D_MODEL)),
        'w_in': nrm(ks[6], (DEPTH, D_MODEL, IN_WIDTH)) * D_MODEL ** -0.5,
        'na_rpb': 0.1 * nrm(ks[7], (DEPTH, NA_HEADS, 2 * NA_WIN_ROWS - 1, 2 * NA_WIN_COLS - 1)),
        'diff_lambda': 0.1 * nrm(ks[8], (DEPTH, 4, DIFF_QK_DIM)),
        'diff_subln_g': 1.0 + 0.02 * nrm(ks[9], (DEPTH, HEAD_DIM)),
        'w_out': nrm(ks[10], (DEPTH, MIX_WIDTH, D_MODEL)) * (MIX_WIDTH ** -0.5) * DEEPNORM_BETA,
        'ln1_g': 1.0 + 0.02 * nrm(ks[11], (DEPTH, D_MODEL)),
        'ln1_b': 0.02 * nrm(ks[12], (DEPTH, D_MODEL)),
        'w_gu': nrm(ks[13], (DEPTH, D_MODEL, 2 * FFN_HIDDEN)) * D_MODEL ** -0.5,
        'w_down': nrm(ks[14], (DEPTH, FFN_HIDDEN, D_MODEL)) * (FFN_HIDDEN ** -0.5) * DEEPNORM_BETA,
        'ln2_g': 1.0 + 0.02 * nrm(ks[15], (DEPTH, D_MODEL)),
        'ln2_b': 0.02 * nrm(ks[16], (DEPTH, D_MODEL)),
    }


def reference(x_prompt, x_sample, c_prompt, c_sample, w_ada, b_ada, w_in, na_rpb, diff_lambda,
              diff_subln_g, w_out, ln1_g, ln1_b, w_gu, w_down, ln2_g, ln2_b):
    y_prompt = encoder_trunk(x_prompt, c_prompt, w_ada, b_ada, w_in, na_rpb, diff_lambda, diff_subln_g,
                             w_out, ln1_g, ln1_b, w_gu, w_down, ln2_g, ln2_b)
    y_sample = encoder_trunk(x_sample, c_sample, w_ada, b_ada, w_in, na_rpb, diff_lambda, diff_subln_g,
                             w_out, ln1_g, ln1_b, w_gu, w_down, ln2_g, ln2_b)
    return (y_prompt, y_sample)
```

```python
import math
import numpy as np
from contextlib import ExitStack
import concourse.bass as bass
import concourse.mybir as mybir
from concourse.bass_utils import run_bass_kernel_spmd

F32 = mybir.dt.float32
BF16 = mybir.dt.bfloat16
AF = mybir.ActivationFunctionType
ALU = mybir.AluOpType

D = 1024
NFF = 2816
NFC = NFF // 128
WIN = 3072
ALPHA = 4.0 ** 0.25
EPS = 1e-5
THETA = 10000.0
import os as _os
MASK_ENG = _os.environ.get("MASK_ENG", "dve")


class Buf:
    __slots__ = ("name", "w", "r", "dsem", "dcnt", "phase")

    def __init__(self, name=""):
        self.name = name
        self.w = {}
        self.r = {}
        self.dsem = None
        self.dcnt = 0
        self.phase = -1


class KB:
    ENG = ("pe", "act", "dve", "pool", "sp")

    def __init__(self, nc, stack):
        self.nc = nc
        self.stack = stack
        self.q = {e: [] for e in self.ENG}
        self.cnt = {e: 0 for e in self.ENG}
        self.seen = {e: {} for e in self.ENG}
        self.sems = {e: stack.enter_context(nc.semaphore("s_" + e)) for e in self.ENG}
        self.ndsem = 0
        self.dma_bufs = []
        self.ninstr = 0
        self.phase = 0
        self.free_dsems = []

    def _gather(self, eng, reads, writes, strict_own=False, own=True):
        need = {}
        for b in reads:
            for k, v in b.w.items():
                if need.get(k, 0) < v:
                    need[k] = v
        for b in writes:
            for k, v in b.w.items():
                if need.get(k, 0) < v:
                    need[k] = v
            for k, v in b.r.items():
                if need.get(k, 0) < v:
                    need[k] = v
        waits = []
        seen = self.seen[eng]
        for k, v in need.items():
            if not isinstance(k, str) and k.phase != self.phase:
                continue
            if k == eng and (not own or v > self.cnt[eng] or (not strict_own and (eng in ("pe", "sp") or v < self.cnt[eng] - 1))):
                continue
            if seen.get(k, 0) >= v:
                continue
            seen[k] = v
            waits.append((k, v))
        return waits

    def _semof(self, k):
        return self.sems[k] if isinstance(k, str) else k.dsem

    def op(self, eng, fn, reads=(), writes=(), inc=True, own=True):
        waits = self._gather(eng, reads, writes, own=own)
        semo = self.sems[eng]
        wl = [(self._semof(k), v) for k, v in waits]
        if inc:
            self.cnt[eng] += 1
            val = self.cnt[eng]
        else:
            val = self.cnt[eng] + 1

        def emit(e, fn=fn, wl=wl, inc=inc, semo=semo):
            for s, v in wl:
                e.wait_ge(s, v)
            ins = fn(e)
            if inc:
                ins.then_inc(semo, 1)
        self.q[eng].append(emit)
        self.ninstr += 1 + len(wl)
        for b in writes:
            b.w = {eng: val}
            b.r = {}
        for b in reads:
            if b.r.get(eng, 0) < val:
                b.r[eng] = val
        return val

    def dma(self, queue, out, in_, key, reads=(), writes=(), **kw):
        if key.dsem is None or key.phase != self.phase:
            if self.free_dsems:
                key.dsem, key.dcnt = self.free_dsems.pop()
            else:
                key.dsem = self.stack.enter_context(self.nc.semaphore("d%d" % self.ndsem))
                key.dcnt = 0
                self.ndsem += 1
            key.phase = self.phase
            self.dma_bufs.append(key)
        waits = self._gather(queue, reads, writes, strict_own=True)
        wl = [(self._semof(k), v) for k, v in waits]
        key.dcnt += 16
        val = key.dcnt
        sem = key.dsem

        def emit(e, out=out, in_=in_, wl=wl, sem=sem, kw=kw):
            for s, v in wl:
                e.wait_ge(s, v)
            e.dma_start(out=out, in_=in_, **kw).then_inc(sem, 16)
        self.q[queue].append(emit)
        self.ninstr += 1 + len(wl)
        for b in writes:
            b.w = {key: val}
            b.r = {}
        for b in reads:
            if b.r.get(key, 0) < val:
                b.r[key] = val
        return val

    def flush(self):
        nc = self.nc
        fin = {e: [] for e in self.ENG}
        for b in self.dma_bufs:
            if b.dcnt and self.seen["sp"].get(b, 0) < b.dcnt:
                fin["sp"].append((b.dsem, b.dcnt))
                self.seen["sp"][b] = b.dcnt
        self.cnt["sp"] += 1
        spv = self.cnt["sp"]
        sps = self.sems["sp"]
        for e in self.ENG:
            for o in self.ENG:
                if o != e and self.cnt[o] and self.seen[e].get(o, 0) < self.cnt[o]:
                    fin[e].append((self.sems[o], self.cnt[o]))
                    self.seen[e][o] = self.cnt[o]
        q = self.q

        def mk(e):
            def run(eng):
                for f in q[e]:
                    f(eng)
                if e == "sp":
                    for s, v in fin[e]:
                        if s is not sps:
                            eng.wait_ge(s, v)
                    eng.nop().then_inc(sps, 1)
                else:
                    for s, v in fin[e]:
                        eng.wait_ge(s, v)
            return run
        with nc.Block() as block:
            block.tensor(mk("pe"))
            block.scalar(mk("act"))
            block.vector(mk("dve"))
            block.gpsimd(mk("pool"))
            block.sync(mk("sp"))
        self.q = {e: [] for e in self.ENG}
        for b in self.dma_bufs:
            self.free_dsems.append((b.dsem, b.dcnt))
        dead = set(id(b) for b in self.dma_bufs)
        self._dead = getattr(self, "_dead", set()) | dead
        for e in self.ENG:
            self.seen[e] = {k: v for k, v in self.seen[e].items() if isinstance(k, str)}
        self.dma_bufs = []
        self.phase += 1


def _win_perm():
    qa = list(range(0, 256)); ka = list(range(256, 512)); va = list(range(512, 768))
    qb = list(range(768, 1024)); kb_ = list(range(1024, 1280)); vb = list(range(1280, 1536))
    qc = list(range(1536, 2048)); kc = list(range(2048, 2560)); vc = list(range(2560, 3072))
    A = qb + kb_ + qc + kc

    def swap(cols, hd):
        out = []
        for i in range(0, len(cols), hd):
            blk = cols[i:i + hd]
            out += blk[hd // 2:] + blk[:hd // 2]
        return out
    Bc = swap(qb, 32) + swap(kb_, 32) + swap(qc, 64) + swap(kc, 64)
    return np.array(qa + ka + A + va + vb + vc, dtype=np.int64)


def _rope_tables(smax):
    f32 = np.float32
    t = np.arange(smax, dtype=f32)

    def tab(half, reps):
        inv = (f32(THETA) ** (-(np.arange(half, dtype=f32) / f32(half)))).astype(f32)
        ang = (t[None, :] * inv[:, None]).astype(f32)
        c = np.cos(ang).astype(f32); s = np.sin(ang).astype(f32)
        cos = np.concatenate([c, c], 0)
        sins = np.concatenate([-s, s], 0)
        return np.tile(cos, (reps, 1)), np.tile(sins, (reps, 1))
    cb, sb = tab(16, 4)
    cc, sc = tab(32, 2)
    return np.ascontiguousarray(np.stack([cb, sb, cc, sc], 0))


def _consts():
    ident = np.eye(128, dtype=np.float32)
    cq = np.arange(64)
    cs = np.clip(cq - 8, 0, 48)
    colin = (cq[None, :] >= cs[:, None]) & (cq[None, :] < cs[:, None] + 16)
    colmaskT = colin.T.astype(np.float32)
    colmask2 = np.concatenate([colmaskT, colmaskT], 0)
    k = np.arange(128)[:, None]; q = np.arange(128)[None, :]
    band = np.stack([(np.abs(128 * o + k - q) <= 64).astype(np.float32) for o in (-1, 0, 1)], 1)
    bd = np.zeros((128, 128), np.float32); bd[:64, :64] = 1; bd[64:, 64:] = 1
    perms = np.zeros((2, 128, 128), np.float32)
    for t_, hd in enumerate((32, 64)):
        for m in range(128):
            blk, r = divmod(m, hd)
            perms[t_, blk * hd + (r + hd // 2) % hd, m] = 1.0
    return ident, colmask2, np.ascontiguousarray(band), bd, perms


def build(seqs, depth, dbg=False, phases=None):
    NB = len(seqs)
    T = sum(seqs)
    SMAX = max(seqs)
    seq_off = [sum(seqs[:i]) for i in range(NB)]
    nc = bass.Bass("TRN2", target_bir_lowering=False)

    def din(name, shape, dt=F32):
        return nc.dram_tensor(name, list(shape), dt, kind="ExternalInput")

    def dscr(name, shape, dt):
        if dbg and name in ("qkT", "vaug", "mixT", "xmid", "zz_d", "gbc", "lnbc"):
            return nc.dram_tensor(name, list(shape), dt, kind="ExternalOutput")
        return nc.dram_tensor(name, list(shape), dt)

    x_in = din("x", [T, D]); cT_in = din("cT", [128, 8, NB])
    wada_in = din("w_ada", [depth, D, 6 * D]); bada_in = din("b_ada", [depth, 6 * D])
    win_in = din("w_in", [depth, D, WIN]); rpb_in = din("rpb", [depth, 60, 31])
    lam_in = din("lam", [depth, 128]); subg_in = din("subg", [depth, 128])
    wout_in = din("w_out", [depth, D, D]); wgu_in = din("w_gu", [depth, D, 2 * NFF]); wdn_in = din("w_down", [depth, NFF, D])
    lnp_in = din("lnp", [depth, 4, D])
    rope_in = din("rope", [4, 128, SMAX]); ident_in = din("ident", [128, 128]); colmask_in = din("colmask", [128, 64])
    band_in = din("band", [128, 3, 128]); bd_in = din("bd", [128, 128]); perms_in = din("perms", [2, 128, 128])
    y_out = nc.dram_tensor("y", [T, D], F32, kind="ExternalOutput")

    win_bf = dscr("win_bf", [depth, 128, 8, WIN], BF16)
    wout_bf = dscr("wout_bf", [depth, 128, 8, D], BF16)
    wgu_bf = dscr("wgu_bf", [depth, 128, 8, 2 * NFF], BF16)
    wdn_bf = dscr("wdn_bf", [depth, 128, NFC, D], BF16)
    gbc = dscr("gbc", [depth, NB, 2, 128, D], F32)
    lnbc = dscr("lnbc", [depth, 4, 128, D], F32)
    gpad = dscr("gpad", [depth, 61, 127], F32)
    zz_d = dscr("zz_d", [depth, 128, 56 * 64], BF16)
    qkT = dscr("qkT", [16, 128, T], BF16)
    vaug = dscr("vaug", [T, 16, 128], BF16)
    mixT = dscr("mixT", [8, 128, T], BF16)
    xmid = dscr("xmid", [T, D], F32)
    xlay = dscr("xlay", [T, D], F32)

    dbg_outs = {}
    with ExitStack() as top:
        kb = KB(nc, top)

        uid = [0]

        def sbt(st, name, shape, dt=F32):
            uid[0] += 1
            return st.enter_context(nc.sbuf_tensor("sb%d_%s" % (uid[0], name), list(shape), dt))

        def pst(st, name, shape, dt=F32):
            uid[0] += 1
            return st.enter_context(nc.psum_tensor("ps%d_%s" % (uid[0], name), list(shape), dt))

        ident = sbt(top, "ident", [128, 128]); identb = sbt(top, "identb", [128, 128], BF16)
        ones_row = sbt(top, "ones_row", [1, 128]); bdones = sbt(top, "bdones", [128, 128]); permt = sbt(top, "permt", [128, 2, 128])
        modT = sbt(top, "modT", [128, depth, 4, 8, NB])
        neglam = sbt(top, "neglam", [128, depth]); gsub = sbt(top, "gsub", [128, depth])
        epst = sbt(top, "epst", [128, 1])
        bandm = sbt(top, "bandm", [128, 3, 128], BF16)
        B_const = Buf("const")
        B_modT = Buf("modT"); B_small = Buf("small")

        with ExitStack() as st:
            kb.dma("sp", ident[:], ident_in.ap()[:, :], key=B_const, writes=[B_const])
            kb.dma("sp", bdones[:], bd_in.ap()[:, :], key=B_const, writes=[B_const])
            kb.dma("sp", permt[:], perms_in.ap().rearrange("t k m -> k t m"), key=B_const, writes=[B_const])
            kb.op("dve", lambda e: e.tensor_copy(out=identb[:], in_=ident[:]), reads=[B_const], writes=[B_const])
            kb.op("dve", lambda e: e.memset(ones_row[:], 1.0), writes=[B_const])
            kb.op("dve", lambda e: e.memset(epst[:], EPS), writes=[B_const])
            bandf = sbt(st, "bandf", [128, 3, 128]); B_bandf = Buf()
            kb.dma("sp", bandf[:], band_in.ap()[:, :, :], key=B_bandf, writes=[B_bandf])
            kb.op("dve", lambda e: e.tensor_copy(out=bandm[:], in_=bandf[:]), reads=[B_bandf], writes=[B_const])

            stg = [sbt(st, "stg%d" % i, [128, 5632]) for i in range(2)]
            stb = [sbt(st, "stb%d" % i, [128, 5632], BF16) for i in range(2)]
            Bstg = [Buf() for _ in range(2)]; Bstb = [Buf() for _ in range(2)]
            cnt = [0]

            def conv(src_ap, dst_ap, ncols):
                i = cnt[0] % 2; cnt[0] += 1
                kb.dma("sp", stg[i][:, 0:ncols], src_ap, key=Bstg[i], writes=[Bstg[i]])
                eng = ("dve", "pool", "act")[cnt[0] % 3]
                if eng == "act":
                    kb.op("act", lambda e: e.copy(out=stb[i][:, 0:ncols], in_=stg[i][:, 0:ncols]), reads=[Bstg[i]], writes=[Bstb[i]])
                else:
                    kb.op(eng, lambda e: e.tensor_copy(out=stb[i][:, 0:ncols], in_=stg[i][:, 0:ncols]), reads=[Bstg[i]], writes=[Bstb[i]])
                kb.dma("pool", dst_ap, stb[i][:, 0:ncols], key=Bstb[i], reads=[Bstb[i]])
            for l in range(depth):
                for kc in range(8):
                    conv(win_in.ap()[l, kc * 128:(kc + 1) * 128, :], win_bf.ap()[l, :, kc, :], WIN)
                    conv(wgu_in.ap()[l, kc * 128:(kc + 1) * 128, :], wgu_bf.ap()[l, :, kc, :], 2 * NFF)
                    conv(wout_in.ap()[l, kc * 128:(kc + 1) * 128, :], wout_bf.ap()[l, :, kc, :], D)
                for fc in range(NFC):
                    conv(wdn_in.ap()[l, fc * 128:(fc + 1) * 128, :], wdn_bf.ap()[l, :, fc, :], D)

            kb.flush()
        with ExitStack() as st:
            cT = sbt(st, "cT", [128, 8, NB]); siluT = sbt(st, "siluT", [128, 8, NB]); B_c = Buf()
            silubc = sbt(st, "silubc", [128, NB, 8, 128])
            kb.dma("sp", cT[:], cT_in.ap()[:, :, :], key=B_c, writes=[B_c])
            kb.op("act", lambda e: e.activation(out=siluT[:], in_=cT[:], func=AF.Silu), reads=[B_c], writes=[B_c])
            for b in range(NB):
                kb.op("dve", lambda e, b=b: e.tensor_copy(out=silubc[:, b, :, :], in_=siluT[:, :, b:b + 1].broadcast_to([128, 8, 128])),
                      reads=[B_c], writes=[B_c])
            ones_nb = sbt(st, "ones_nb", [1, 128])
            kb.op("dve", lambda e: e.memset(ones_nb[:], 1.0), writes=[B_c])
            wa = [sbt(st, "wa%d" % i, [128, 8, 1024]) for i in range(2)]; Bwa = [Buf() for _ in range(2)]
            brow = [sbt(st, "brow%d" % i, [1, 1024]) for i in range(2)]; Bbrow = [Buf() for _ in range(2)]
            pm = [pst(st, "pm%d" % i, [128, 512]) for i in range(2)]; Bpm = [Buf() for _ in range(2)]
            gst = [sbt(st, "gst%d" % i, [128, 1024]) for i in range(2)]; Bgst = [Buf() for _ in range(2)]
            it = 0
            for l in range(depth):
                for kind in range(6):
                    i = it % 2; it += 1
                    kb.dma("sp", wa[i][:], wada_in.ap()[l, :, kind * 1024:(kind + 1) * 1024].rearrange("(kc p) n -> p kc n", p=128),
                           key=Bwa[i], writes=[Bwa[i]])
                    kb.dma("sp", brow[i][:], bada_in.ap()[l:l + 1, kind * 1024:(kind + 1) * 1024], key=Bbrow[i], writes=[Bbrow[i]])
                    if kind in (0, 1, 3, 4):
                        slot = {0: 0, 1: 1, 3: 2, 4: 3}[kind]
                        pmt = pm[0]

                        def mm_fm(e, i=i, pmt=pmt):
                            for ncn in range(8):
                                for kc in range(8):
                                    e.matmul(pmt[:, ncn * NB:(ncn + 1) * NB], lhsT=wa[i][:, kc, ncn * 128:(ncn + 1) * 128],
                                             rhs=siluT[:, kc, :], start=(kc == 0), stop=False)
                                m = e.matmul(pmt[:, ncn * NB:(ncn + 1) * NB], lhsT=brow[i][0:1, ncn * 128:(ncn + 1) * 128],
                                             rhs=ones_nb[0:1, 0:NB], start=False, stop=True)
                            return m
                        kb.op("pe", mm_fm, reads=[Bwa[i], Bbrow[i], B_c], writes=[Bpm[0]])
                        addv = 1.0 if kind in (1, 4) else 0.0
                        kb.op("dve", lambda e, l=l, slot=slot, addv=addv, pmt=pmt: e.tensor_scalar(
                            out=modT[:, l, slot, :, :], in0=pmt[:, 0:8 * NB].rearrange("p (a b) -> p a b", b=NB),
                            scalar1=addv, scalar2=None, op0=ALU.add), reads=[Bpm[0]], writes=[Bpm[0], B_modT])
                    else:
                        which = 0 if kind == 2 else 1
                        for b in range(NB):
                            gi = (b + which) % 2
                            for hf in range(2):
                                def mm_bc(e, i=i, b=b, hf=hf):
                                    for kc in range(8):
                                        e.matmul(pm[1][:, :], lhsT=silubc[:, b, kc, :], rhs=wa[i][:, kc, hf * 512:(hf + 1) * 512],
                                                 start=(kc == 0), stop=False)
                                    return e.matmul(pm[1][:, :], lhsT=ones_row[0:1, :], rhs=brow[i][0:1, hf * 512:(hf + 1) * 512],
                                                    start=False, stop=True)
                                kb.op("pe", mm_bc, reads=[Bwa[i], Bbrow[i], B_c, B_const], writes=[Bpm[1]])
                                kb.op("act", lambda e, gi=gi, hf=hf: e.copy(out=gst[gi][:, hf * 512:(hf + 1) * 512], in_=pm[1][:, :]),
                                      reads=[Bpm[1]], writes=[Bpm[1], Bgst[gi]])
                            kb.dma("pool", gbc.ap()[l, b, which, :, :], gst[gi][:], key=Bgst[gi], reads=[Bgst[gi]])
            kb.flush()
        with ExitStack() as st:
            pm = [pst(st, "pmb%d" % i, [128, 512]) for i in range(2)]; Bpm = [Buf() for _ in range(2)]
            gst = [sbt(st, "gstb%d" % i, [128, 1024]) for i in range(2)]; Bgst = [Buf() for _ in range(2)]
            lrow = sbt(st, "lrow", [1, depth * 4 * D]); B_lrow = Buf()
            kb.dma("sp", lrow[:], lnp_in.ap().rearrange("l k d -> (l k d)").rearrange("(o n) -> o n", o=1), key=B_lrow, writes=[B_lrow])
            for l in range(depth):
                for k4 in range(4):
                    gi = (l * 4 + k4) % 2
                    for hf in range(2):
                        off = (l * 4 + k4) * D + hf * 512
                        kb.op("pe", lambda e, off=off: e.matmul(pm[1][:, :], lhsT=ones_row[0:1, :], rhs=lrow[0:1, off:off + 512],
                                                                 start=True, stop=True), reads=[B_lrow, B_const], writes=[Bpm[1]])
                        kb.op("act", lambda e, gi=gi, hf=hf: e.copy(out=gst[gi][:, hf * 512:(hf + 1) * 512], in_=pm[1][:, :]),
                              reads=[Bpm[1]], writes=[Bpm[1], Bgst[gi]])
                    kb.dma("pool", lnbc.ap()[l, k4, :, :], gst[gi][:], key=Bgst[gi], reads=[Bgst[gi]])
            lamr = sbt(st, "lamr", [1, depth, 128]); lwk = sbt(st, "lwk", [1, depth, 8]); B_lam = Buf()
            kb.dma("sp", lamr[:], lam_in.ap().rearrange("(o l) n -> o l n", o=1), key=B_lam, writes=[B_lam])
            sg = sbt(st, "sg", [128, depth]); B_sg = Buf()
            kb.dma("sp", sg[:], subg_in.ap().rearrange("l p -> p l"), key=B_sg, writes=[B_sg], allow_slow_non_contiguous=True)
            for l in range(depth):
                linit = 0.8 - 0.6 * math.exp(-0.3 * l)
                prod = sbt(st, "prod%d" % l, [1, 64])
                kb.op("dve", lambda e, l=l, prod=prod: e.tensor_tensor(
                    out=prod[:].rearrange("o (a b) -> o a b", a=2), in0=lamr[0:1, l, :].rearrange("o (a c b) -> o a c b", a=2, c=2)[:, :, 0, :],
                    in1=lamr[0:1, l, :].rearrange("o (a c b) -> o a c b", a=2, c=2)[:, :, 1, :], op=ALU.mult), reads=[B_lam], writes=[B_lam])
                kb.op("dve", lambda e, l=l, prod=prod: e.reduce_sum(out=lwk[0:1, l, 0:2], in_=prod[:].rearrange("o (a b) -> o a b", a=2),
                                                                    axis=mybir.AxisListType.X), reads=[B_lam], writes=[B_lam])
                kb.op("act", lambda e, l=l: e.activation(out=lwk[0:1, l, 2:4], in_=lwk[0:1, l, 0:2], func=AF.Exp), reads=[B_lam], writes=[B_lam])
                kb.op("dve", lambda e, l=l: e.tensor_tensor(out=lwk[0:1, l, 4:5], in0=lwk[0:1, l, 3:4], in1=lwk[0:1, l, 2:3], op=ALU.subtract),
                      reads=[B_lam], writes=[B_lam])
                kb.op("dve", lambda e, l=l, linit=linit: e.tensor_scalar(out=lwk[0:1, l, 5:6], in0=lwk[0:1, l, 4:5], scalar1=-linit, scalar2=None, op0=ALU.add),
                      reads=[B_lam], writes=[B_lam])
                kb.op("pe", lambda e, l=l: e.matmul(pm[1][:, 0:1], lhsT=ones_row[0:1, :], rhs=lwk[0:1, l, 5:6], start=True, stop=True),
                      reads=[B_lam, B_const], writes=[Bpm[1]])
                kb.op("act", lambda e, l=l: e.copy(out=neglam[:, l:l + 1], in_=pm[1][:, 0:1]), reads=[Bpm[1]], writes=[Bpm[1], B_small])
                kb.op("act", lambda e, l=l, linit=linit: e.mul(out=gsub[:, l:l + 1], in_=sg[:, l:l + 1], mul=(1.0 - linit)), reads=[B_sg], writes=[B_small])
            kb.flush()
        with ExitStack() as st:
            ZZ = sbt(st, "ZZ", [128, depth, 4, 14, 64], BF16); B_ZZ = Buf()
            rp = sbt(st, "rp", [61, depth, 31]); gp = sbt(st, "gp", [61, depth, 127]); B_rp = Buf(); B_gpad = Buf()
            hk = sbt(st, "hk", [128, depth, 4, 14, 64]); cmask = sbt(st, "cmask", [128, 64]); B_hk = Buf()
            kb.dma("sp", cmask[:], colmask_in.ap()[:, :], key=B_hk, writes=[B_hk])
            kb.op("pool", lambda e: e.memset(gp[:], 0.0), writes=[B_rp])
            kb.op("pool", lambda e: e.memset(rp[:], 0.0), writes=[B_rp])
            for l in range(depth):
                kb.dma("sp", rp[0:60, l, :], rpb_in.ap()[l, :, :], key=B_rp, writes=[B_rp])
            kb.op("act", lambda e: e.activation(out=gp[0:60, :, 48:79], in_=rp[0:60, :, :], func=AF.Exp), reads=[B_rp], writes=[B_rp])
            for l in range(depth):
                kb.dma("pool", gpad.ap()[l, :, :], gp[:, l, :], key=B_rp, reads=[B_rp], writes=[B_gpad])
            for l in range(depth):
                for h in range(4):
                    for half in range(2):
                        src = bass.AP(gpad, (l * 61 + h * 15 + half) * 127, [[1, 64], [127, 14], [1, 64]])
                        kb.dma("sp", hk[half * 64:(half + 1) * 64, l, h, :, :], src, key=B_hk, reads=[B_gpad], writes=[B_hk])
            for l in range(depth):
                kb.op("dve", lambda e, l=l: e.tensor_tensor(
                    out=ZZ[:, l, :, :, :].rearrange("p h d c -> p (h d) c"), in0=hk[:, l, :, :, :].rearrange("p h d c -> p (h d) c")[:, :, ::-1],
                    in1=cmask[:, :].unsqueeze(1).broadcast_to([128, 56, 64]), op=ALU.mult), reads=[B_hk], writes=[B_ZZ])
                kb.dma("pool", zz_d.ap()[l, :, :], ZZ[:, l, :, :, :].rearrange("p h d c -> p (h d c)"), key=B_ZZ, reads=[B_ZZ])
            kb.flush()

        for l in range(depth):
            x_src = x_in if l == 0 else xlay
            x_dst = y_out if l == depth - 1 else xlay
            ph = (lambda n: phases is None or n in phases)
            if ph("P"):
                phase_P(nc, kb, sbt, pst, l, seqs, seq_off, NB, x_src, win_bf, rope_in, modT, B_modT, ident, B_const, qkT, vaug, permt)
            if ph("A1"):
                phase_A1(nc, kb, sbt, pst, l, seqs, seq_off, qkT, vaug, mixT, zz_d)
            if ph("A2"):
                phase_A2(nc, kb, sbt, pst, l, seqs, seq_off, qkT, vaug, mixT, neglam, gsub, epst, bdones, B_small, B_const)
            if ph("A3"):
                phase_A3(nc, kb, sbt, pst, l, seqs, seq_off, qkT, vaug, mixT, bandm, B_const)
            if ph("F1"):
                phase_F1(nc, kb, sbt, pst, l, seqs, seq_off, NB, x_src, mixT, wout_bf, gbc, lnbc, epst, B_const, xmid)
            if ph("F2"):
                phase_F2(nc, kb, sbt, pst, l, seqs, seq_off, NB, xmid, wgu_bf, wdn_bf, gbc, lnbc, modT, B_modT, ident, epst, B_const, x_dst)
    return nc, kb


def _ln_tile(kb, yt, By, stt, Bst, epst, B_const, g_bc, b_bc, B_bc, outt, Bout, tagn):
    kb.op("dve", lambda e: e.bn_stats(out=stt[:, 0:6], in_=yt[:, 0:512]), reads=[By], writes=[Bst])
    kb.op("dve", lambda e: e.bn_stats(out=stt[:, 6:12], in_=yt[:, 512:1024]), reads=[By], writes=[Bst])
    kb.op("dve", lambda e: e.bn_aggr(out=stt[:, 12:14], in_=stt[:, 0:12]), reads=[Bst], writes=[Bst])
    kb.op("act", lambda e: e.activation(out=stt[:, 14:15], in_=stt[:, 13:14], func=AF.Sqrt, bias=epst[:, 0:1], scale=1.0),
          reads=[Bst, B_const], writes=[Bst])
    kb.op("dve", lambda e: e.reciprocal(out=stt[:, 15:16], in_=stt[:, 14:15]), reads=[Bst], writes=[Bst])
    kb.op("dve", lambda e: e.scalar_tensor_tensor(out=stt[:, 16:17], in0=stt[:, 12:13], scalar=-1.0, in1=stt[:, 15:16], op0=ALU.mult, op1=ALU.mult),
          reads=[Bst], writes=[Bst])
    kb.op("act", lambda e: e.activation(out=yt[:, :], in_=yt[:, :], func=AF.Identity, scale=stt[:, 15:16], bias=stt[:, 16:17]),
          reads=[Bst, By], writes=[By])
    kb.op("pool", lambda e: e.tensor_tensor(out=yt[:, :], in0=yt[:, :], in1=g_bc[:, :], op=ALU.mult), reads=[By, B_bc], writes=[By])
    kb.op("pool", lambda e: e.tensor_tensor(out=outt[:, :], in0=yt[:, :], in1=b_bc[:, :], op=ALU.add), reads=[By, B_bc], writes=[Bout])


def phase_P(nc, kb, sbt, pst, l, seqs, seq_off, NB, x_src, win_bf, rope_in, modT, B_modT, ident, B_const, qkT, vaug, permt):
    with ExitStack() as st:
        w = sbt(st, "P_w", [128, 8, WIN], BF16); Bw = Buf()
        for kc in range(8):
            kb.dma("sp", w[:, kc, :], win_bf.ap()[l, :, kc, :], key=Bw, writes=[Bw])
        xt = [sbt(st, "P_x%d" % i, [128, D]) for i in range(3)]; Bx = [Buf() for _ in range(3)]
        hT = [sbt(st, "P_hT%d" % i, [128, 8, 512], BF16) for i in range(2)]; BhT = [[Buf(), Buf()] for _ in range(2)]
        tab = [sbt(st, "P_tab%d" % i, [128, 4, 512]) for i in range(2)]; Btab = [Buf() for _ in range(2)]
        pT = [pst(st, "P_pT%d" % i, [128, 1024]) for i in range(1)]; BpT = [Buf(), Buf()]
        pc = [pst(st, "P_pc%d" % i, [128, 512]) for i in range(4)]; Bpc = [Buf() for _ in range(4)]
        pv = pst(st, "P_pv", [128, 1024]); Bpv = Buf(); Bpvh = [Buf(), Buf()]
        ob = [sbt(st, "P_ob%d" % i, [128, 512], BF16) for i in range(4)]; Bob = [Buf() for _ in range(4)]
        m1 = [sbt(st, "P_m1%d" % i, [128, 512]) for i in range(2)]; Bm1 = [Buf() for _ in range(2)]
        m2 = [sbt(st, "P_m2%d" % i, [128, 512]) for i in range(2)]; Bm2 = [Buf() for _ in range(2)]
        asb = [sbt(st, "P_asb%d" % i, [128, 512]) for i in range(2)]; Basb = [Buf() for _ in range(2)]
        va = [sbt(st, "P_va%d" % i, [128, 16, 128], BF16) for i in range(2)]; Bva = [Buf() for _ in range(2)]
        for i in range(2):
            kb.op("pool", lambda e, i=i: e.memset(va[i][:], 1.0), writes=[Bva[i]])
        blocks = [(b, t0) for b in range(NB) for t0 in range(0, seqs[b], 512)]
        cnt = {"xi": 0, "oi": 0, "mi": 0, "vi": 0}

        def prep_tile(bi, tt):
            b, t0 = blocks[bi]
            g0 = seq_off[b] + t0
            hb = bi % 2
            if tt == 0:
                kb.dma("sp", tab[hb][:], rope_in.ap()[:, :, t0:t0 + 512].rearrange("k p t -> p k t"), key=Btab[hb], writes=[Btab[hb]])
            xb = cnt["xi"] % 3; cnt["xi"] += 1
            kb.dma("sp", xt[xb][:], x_src.ap()[g0 + tt * 128:g0 + (tt + 1) * 128, :], key=Bx[xb], writes=[Bx[xb]])

            def tr(e, xb=xb):
                for kc in range(8):
                    m = e.transpose(pT[0][:, kc * 128:(kc + 1) * 128], xt[xb][:, kc * 128:(kc + 1) * 128], ident[:])
                return m
            kb.op("pe", tr, reads=[Bx[xb], B_const], writes=[BpT[0], BpT[1]])
            for kc in range(8):
                if kc < 4:
                    kb.op("act", lambda e, kc=kc: e.activation(
                        out=hT[hb][:, kc, tt * 128:(tt + 1) * 128], in_=pT[0][:, kc * 128:(kc + 1) * 128], func=AF.Identity,
                        scale=modT[:, l, 1, kc, b:b + 1], bias=modT[:, l, 0, kc, b:b + 1]),
                        reads=[B_modT], writes=[BpT[0], BhT[hb][0]], own=False)
                else:
                    kb.op("dve", lambda e, kc=kc: e.tensor_scalar(
                        out=hT[hb][:, kc, tt * 128:(tt + 1) * 128], in0=pT[0][:, kc * 128:(kc + 1) * 128],
                        scalar1=modT[:, l, 1, kc, b:b + 1], scalar2=modT[:, l, 0, kc, b:b + 1], op0=ALU.mult, op1=ALU.add),
                        reads=[B_modT], writes=[BpT[1], BhT[hb][1]], own=False)

        for tt in range(4):
            prep_tile(0, tt)
        for bi, (b, t0) in enumerate(blocks):
            g0 = seq_off[b] + t0
            hb = bi % 2
            ngrp = 0

            def hook():
                if bi + 1 < len(blocks) and ngrp in (3, 7, 11, 15):
                    prep_tile(bi + 1, (ngrp - 3) // 4)

            def fm(e, pcx, col0, hb=hb):
                for kc in range(8):
                    m = e.matmul(pcx[:, :], lhsT=w[:, kc, col0:col0 + 128], rhs=hT[hb][:, kc, :], start=(kc == 0), stop=(kc == 7))
                return m
            for ch in range(4):
                pi = ch % 4; o = cnt["oi"] % 4; cnt["oi"] += 1
                kb.op("pe", lambda e, pi=pi, ch=ch, fm=fm: fm(e, pc[pi], ch * 128), reads=[Bw] + BhT[hb], writes=[Bpc[pi]])
                kb.op("act", lambda e, pi=pi, o=o: e.copy(out=ob[o][:, :], in_=pc[pi][:, :]), reads=[], writes=[Bpc[pi], Bob[o]])
                kb.dma("pool", qkT.ap()[ch, :, g0:g0 + 512], ob[o][:, :], key=Bob[o], reads=[Bob[o]])
                hook(); ngrp += 1
            for ch in range(12):
                pa = (2 * ch) % 4; pb = (2 * ch + 1) % 4; o = cnt["oi"] % 4; cnt["oi"] += 1; mm = cnt["mi"] % 2; cnt["mi"] += 1
                ti = 0 if ch < 4 else 2
                pt_ = 0 if ch < 4 else 1
                kb.op("pe", lambda e, pa=pa, ch=ch, fm=fm: fm(e, pc[pa], 512 + ch * 128), reads=[Bw] + BhT[hb], writes=[Bpc[pa]])
                kb.op("act", lambda e, pa=pa, mm=mm: e.copy(out=asb[mm][:, :], in_=pc[pa][:, :]), reads=[], writes=[Bpc[pa], Basb[mm]])
                kb.op("pe", lambda e, pb=pb, mm=mm, pt_=pt_: e.matmul(pc[pb][:, :], lhsT=permt[:, pt_, :], rhs=asb[mm][:, :], start=True, stop=True),
                      reads=[Basb[mm], B_const], writes=[Bpc[pb]])
                kb.op("dve", lambda e, mm=mm, ti=ti, hb=hb: e.tensor_tensor(out=m1[mm][:, :], in0=asb[mm][:, :], in1=tab[hb][:, ti, :], op=ALU.mult),
                      reads=[Btab[hb], Basb[mm]], writes=[Bm1[mm]])
                kb.op("dve", lambda e, pb=pb, mm=mm, ti=ti, hb=hb: e.tensor_tensor(out=m2[mm][:, :], in0=pc[pb][:, :], in1=tab[hb][:, ti + 1, :], op=ALU.mult),
                      reads=[Btab[hb]], writes=[Bpc[pb], Bm2[mm]])
                kb.op("pool", lambda e, mm=mm, o=o: e.tensor_tensor(out=ob[o][:, :], in0=m1[mm][:, :], in1=m2[mm][:, :], op=ALU.add),
                      reads=[Bm1[mm], Bm2[mm]], writes=[Bob[o]])
                kb.dma("pool", qkT.ap()[4 + ch, :, g0:g0 + 512], ob[o][:, :], key=Bob[o], reads=[Bob[o]])
                hook(); ngrp += 1
            for tt in range(4):
                vb_ = cnt["vi"] % 2; cnt["vi"] += 1

                def vm(e, tt=tt, hb=hb):
                    for hf in range(2):
                        for kc in range(8):
                            m = e.matmul(pv[:, hf * 512:(hf + 1) * 512], lhsT=hT[hb][:, kc, tt * 128:(tt + 1) * 128],
                                         rhs=w[:, kc, 2048 + hf * 512:2048 + (hf + 1) * 512], start=(kc == 0), stop=(kc == 7))
                    return m
                kb.op("pe", vm, reads=[Bw] + BhT[hb], writes=[Bpvh[0], Bpvh[1]])
                pvv = pv[:, :].rearrange("p (h two d) -> p h two d", two=2, d=64)
                for two in range(2):
                    cs = slice(0, 64) if two == 0 else slice(64, 128)
                    kb.op("act", lambda e, vb_=vb_, pvv=pvv, two=two, cs=cs: e.copy(out=va[vb_][:, two:8:2, cs], in_=pvv[:, 0:4, two, :]),
                          reads=[], writes=[Bpvh[0], Bva[vb_]], own=False)
                    kb.op("dve", lambda e, vb_=vb_, pvv=pvv, two=two, cs=cs: e.tensor_copy(out=va[vb_][:, 8 + two:16:2, cs], in_=pvv[:, 4:8, two, :]),
                          reads=[], writes=[Bpvh[1], Bva[vb_]], own=False)
                kb.dma("pool", vaug.ap()[g0 + tt * 128:g0 + (tt + 1) * 128, :, :], va[vb_][:, :, :], key=Bva[vb_], reads=[Bva[vb_]])
        kb.flush()


def _pipeline(units, lags=None):
    n = len(units)
    if n == 0:
        return
    ns = len(units[0])
    if lags is None:
        lags = list(range(ns))
    for t in range(n + max(lags)):
        for sidx in range(ns):
            u = t - lags[sidx]
            if 0 <= u < n:
                units[u][sidx]()


def phase_A1(nc, kb, sbt, pst, l, seqs, seq_off, qkT, vaug, mixT, zz_d):
    SM = max(seqs)
    with ExitStack() as st:
        ZZ = sbt(st, "A1_zz", [128, 4, 14, 64], BF16); B_ZZ = Buf()
        kb.dma("sp", ZZ[:].rearrange("p h d c -> p (h d c)"), zz_d.ap()[l, :, :], key=B_ZZ, writes=[B_ZZ])
        ZZv = ZZ[:, :, :, :].rearrange("p (c e) d q -> p e c d q", e=2)
        qT = sbt(st, "A1_q", [128, 2, SM], BF16); kT = sbt(st, "A1_k", [128, 2, SM], BF16); Bq = Buf(); Bk = Buf()
        vt = sbt(st, "A1_v", [128, SM // 128, 4, 128], BF16); Bv = Buf()
        ps = [pst(st, "A1_ps%d" % i, [128, 1024]) for i in range(2)]; Bps = [Buf() for _ in range(2)]
        pa = [pst(st, "A1_pa%d" % i, [128, 4, 64]) for i in range(2)]; Bpa = [Buf() for _ in range(2)]
        ex = [sbt(st, "A1_ex%d" % i, [128, 2, 8, 64], BF16) for i in range(2)]; Bex = [Buf() for _ in range(2)]
        pt = [sbt(st, "A1_pt%d" % i, [128, 2, 8, 64], BF16) for i in range(2)]; Bpt = [Buf() for _ in range(2)]
        rr = [sbt(st, "A1_r%d" % i, [128, 2, 64]) for i in range(2)]; Brr = [Buf() for _ in range(2)]
        mo = [sbt(st, "A1_mo%d" % i, [128, 2, 64], BF16) for i in range(2)]; Bmo = [Buf() for _ in range(2)]
        it = 0
        for b in range(len(seqs)):
            S = seqs[b]; g0 = seq_off[b]; R = S // 64
            for c in range(2):
                kb.dma("sp", qT[:, c, 0:S], qkT.ap()[c, :, g0:g0 + S], key=Bq, writes=[Bq])
                kb.dma("sp", kT[:, c, 0:S], qkT.ap()[2 + c, :, g0:g0 + S], key=Bk, writes=[Bk])
            for par in range(2):
                ntile = S // 128 - par
                kb.dma("sp", vt[:, 0:ntile, :, :], vaug.ap()[g0 + 64 * par:g0 + 64 * par + ntile * 128, 0:4, :].rearrange("(n p) h d -> p n h d", p=128),
                       key=Bv, writes=[Bv])
                units = []
                for r in range(R):
                    rs = min(max(r - 4, 0), R - 8)
                    if rs % 2 != par:
                        continue
                    dl = r - rs
                    i = it % 2; it += 1

                    def s_qk(i=i, r=r, rs=rs):
                        def qk(e):
                            for j in range(4):
                                for c in range(2):
                                    for ee in range(2):
                                        col = ee * 512 + (c * 4 + j) * 64
                                        m = e.matmul(ps[i][:, col:col + 64],
                                                     lhsT=kT[64 * ee:64 * ee + 64, c, rs * 64 + 128 * j:rs * 64 + 128 * (j + 1)],
                                                     rhs=qT[64 * ee:64 * ee + 64, c, r * 64:(r + 1) * 64], start=True, stop=True)
                            return m
                        kb.op("pe", qk, reads=[Bq, Bk], writes=[Bps[i]])

                    def s_mid(i=i, dl=dl):
                        kb.op("act", lambda e: e.activation(out=ex[i][:].rearrange("p a b c -> p (a b c)"), in_=ps[i][:, :], func=AF.Exp, scale=0.125),
                              reads=[], writes=[Bps[i], Bex[i]])
                        for ee in range(2):
                            kb.op(MASK_ENG if ee == 1 else "dve", lambda e, ee=ee: e.tensor_tensor(
                                out=pt[i][:, ee, :, :].rearrange("p (c j) q -> p c j q", c=2), in0=ex[i][:, ee, :, :].rearrange("p (c j) q -> p c j q", c=2),
                                in1=ZZv[:, ee, :, 7 - dl:14 - dl:2, :], op=ALU.mult), reads=[Bex[i], B_ZZ], writes=[Bpt[i]], own=False)

                    def s_pv(i=i, rs=rs, par=par):
                        def pvm(e):
                            tb = (rs - par) // 2
                            for h in range(4):
                                c, ee = h // 2, h % 2
                                for j in range(4):
                                    m = e.matmul(pa[i][:, h, :], lhsT=vt[:, tb + j, h, :], rhs=pt[i][:, ee, c * 4 + j, :], start=(j == 0), stop=(j == 3))
                            return m
                        kb.op("pe", pvm, reads=[Bv, Bpt[i]], writes=[Bpa[i]])

                    def s_post(i=i, r=r, g0=g0):
                        pav = pa[i][:, :, :].rearrange("p (c e) q -> p c e q", e=2)
                        kb.op("act", lambda e: e.activation(out=rr[i][0:64, :, :], in_=pav[64:128, :, 0, :], func=AF.Ln), reads=[], writes=[Bpa[i], Brr[i]])
                        kb.op("act", lambda e: e.activation(out=rr[i][64:128, :, :], in_=pav[0:64, :, 1, :], func=AF.Ln), reads=[], writes=[Bpa[i], Brr[i]], own=False)
                        kb.op("act", lambda e: e.activation(out=rr[i][:, :, :], in_=rr[i][:, :, :], func=AF.Exp, scale=-1.0), reads=[], writes=[Brr[i]])
                        kb.op("dve", lambda e: e.tensor_tensor(out=mo[i][0:64, :, :], in0=pav[0:64, :, 0, :], in1=rr[i][0:64, :, :], op=ALU.mult),
                              reads=[Brr[i]], writes=[Bpa[i], Bmo[i]])
                        kb.op("dve", lambda e: e.tensor_tensor(out=mo[i][64:128, :, :], in0=pav[64:128, :, 1, :], in1=rr[i][64:128, :, :], op=ALU.mult),
                              reads=[Brr[i]], writes=[Bpa[i], Bmo[i]])
                        kb.dma("pool", mixT.ap()[0:2, :, g0 + r * 64:g0 + (r + 1) * 64].rearrange("c p q -> p c q"), mo[i][:, :, :], key=Bmo[i], reads=[Bmo[i]])
                    units.append((s_qk, s_mid, s_pv, s_post))
                _pipeline(units)
        kb.flush()


def phase_A2(nc, kb, sbt, pst, l, seqs, seq_off, qkT, vaug, mixT, neglam, gsub, epst, bdones, B_small, B_const):
    SM = max(seqs)
    sc = 32.0 ** -0.5
    with ExitStack() as st:
        qT = sbt(st, "A2_q", [128, 2, SM], BF16); kT = sbt(st, "A2_k", [128, 2, SM], BF16); Bq = Buf(); Bk = Buf()
        vt = sbt(st, "A2_v", [128, SM // 128, 4, 128], BF16); Bv = Buf()
        ps = [pst(st, "A2_ps%d" % i, [128, 1024]) for i in range(2)]; Bps = [Buf() for _ in range(2)]
        qpad = [sbt(st, "A2_qp%d" % i, [128, 4, 512], BF16) for i in range(2)]; Bqp = [Buf() for _ in range(2)]
        for i in range(2):
            kb.op("pool", lambda e, i=i: e.memset(qpad[i][:], 0.0), writes=[Bqp[i]])
        accs = [[pst(st, "A2_acc%d%d" % (a_, j), [128, 512]) for j in range(2)] for a_ in range(2)]; Baccs = [Buf(), Buf()]
        ai = 0
        pT = [sbt(st, "A2_pT%d" % i, [128, 2, 512], BF16) for i in range(3)]; BpT = [Buf() for _ in range(3)]
        r0 = sbt(st, "A2_r0", [128, 512]); r1 = sbt(st, "A2_r1", [128, 512]); Br = Buf()
        O = [sbt(st, "A2_O%d" % i, [128, 512]) for i in range(2)]; BO = [Buf() for _ in range(2)]
        sq = sbt(st, "A2_sq", [128, 512]); Bsq = Buf()
        mo = [sbt(st, "A2_mo%d" % i, [128, 512], BF16) for i in range(2)]; Bmo = [Buf() for _ in range(2)]
        it = 0; oi = 0
        for b in range(len(seqs)):
            S = seqs[b]; g0 = seq_off[b]; NT = S // 128
            for c in range(2):
                kb.dma("sp", qT[:, c, 0:S], qkT.ap()[4 + c, :, g0:g0 + S], key=Bq, writes=[Bq])
                kb.dma("sp", kT[:, c, 0:S], qkT.ap()[6 + c, :, g0:g0 + S], key=Bk, writes=[Bk])
            kb.dma("sp", vt[:, 0:NT, :, :], vaug.ap()[g0:g0 + S, 4:8, :].rearrange("(n p) h d -> p n h d", p=128), key=Bv, writes=[Bv])
            units = []
            for g in range(2):
                for qc in range(S // 512):
                    o = oi % 2; oi += 1
                    nb = o
                    for ee in range(2):
                        h = 2 * g + ee
                        acc = accs[ai % 2]; Bacc = Baccs[ai % 2]; ai += 1
                        pss = acc[0]; Bss = Bacc
                        for kt in range(NT):
                            i = it % 2; i3 = it % 3; it += 1

                            def s_qk(i=i, kt=kt, ee=ee, g=g, nb=nb, qc=qc):
                                if kt == 0 and ee == 0:
                                    for cidx in range(4):
                                        kb.op("pool", lambda e, cidx=cidx: e.tensor_copy(
                                            out=qpad[nb][32 * cidx:32 * cidx + 32, cidx, :], in_=qT[32 * cidx:32 * cidx + 32, g, qc * 512:(qc + 1) * 512]),
                                            reads=[Bq], writes=[Bqp[nb]])

                                def qk(e):
                                    for mp in range(2):
                                        m = e.matmul(ps[i][:, mp * 512:(mp + 1) * 512], lhsT=kT[:, g, kt * 128:(kt + 1) * 128],
                                                     rhs=qpad[nb][:, 2 * ee + mp, :], start=True, stop=True)
                                    return m
                                kb.op("pe", qk, reads=[Bk, Bqp[nb]], writes=[Bps[i]])

                            def s_mid(i=i, i3=i3):
                                kb.op("act", lambda e: e.activation(out=pT[i3][:].rearrange("p a b -> p (a b)"), in_=ps[i][:, :], func=AF.Exp, scale=sc),
                                      reads=[], writes=[Bps[i], BpT[i3]])

                            def s_pv(i3=i3, kt=kt, h=h, NT=NT, acc=acc, Bacc=Bacc):
                                def pv(e):
                                    for mp in range(2):
                                        m = e.matmul(acc[mp][:, :], lhsT=vt[:, kt, h, :], rhs=pT[i3][:, mp, :], start=(kt == 0), stop=(kt == NT - 1))
                                    return m
                                kb.op("pe", pv, reads=[Bv, BpT[i3]], writes=[Bacc])

                            def s_post(kt=kt, NT=NT, ee=ee, o=o, g=g, qc=qc, g0=g0, acc=acc, Bacc=Bacc, pss=pss, Bss=Bss):
                                if kt != NT - 1:
                                    return
                                import os
                                lvl = int(os.environ.get("A2DBG", "9"))
                                if lvl < 1:
                                    return
                                npr = slice(0, 64) if ee == 0 else slice(64, 128)
                                dpr = slice(64, 128) if ee == 0 else slice(0, 64)
                                kb.op("dve", lambda e: e.reciprocal(out=r0[npr, :], in_=acc[0][dpr, :]), reads=[], writes=[Bacc, Br])
                                kb.op("dve", lambda e: e.reciprocal(out=r1[npr, :], in_=acc[1][dpr, :]), reads=[], writes=[Bacc, Br])
                                kb.op("dve", lambda e: e.tensor_tensor(out=r0[npr, :], in0=acc[0][npr, :], in1=r0[npr, :], op=ALU.mult), reads=[Br], writes=[Bacc, Br])
                                kb.op("dve", lambda e: e.tensor_tensor(out=r1[npr, :], in0=acc[1][npr, :], in1=r1[npr, :], op=ALU.mult), reads=[Br], writes=[Bacc, Br])
                                kb.op("dve", lambda e: e.scalar_tensor_tensor(out=O[o][npr, :], in0=r1[npr, :], scalar=neglam[npr, l:l + 1], in1=r0[npr, :],
                                                                             op0=ALU.mult, op1=ALU.add), reads=[Br, B_small], writes=[BO[o]])
                                if ee == 0 or lvl < 2:
                                    return
                                kb.op("pool", lambda e: e.tensor_tensor(out=sq[:, :], in0=O[o][:, :], in1=O[o][:, :], op=ALU.mult), reads=[BO[o]], writes=[Bsq])
                                kb.op("pe", lambda e: e.matmul(pss[:, :], lhsT=bdones[:, :], rhs=sq[:, :], start=True, stop=True), reads=[Bsq, B_const], writes=[Bss])
                                if lvl < 3:
                                    return
                                kb.op("act", lambda e: e.activation(out=sq[:, :], in_=pss[:, :], func=AF.Ln, bias=epst[:, 0:1], scale=1.0 / 64.0),
                                      reads=[B_const], writes=[Bss, Bsq])
                                kb.op("act", lambda e: e.activation(out=sq[:, :], in_=sq[:, :], func=AF.Exp, scale=-0.5), reads=[], writes=[Bsq])
                                kb.op("dve", lambda e: e.scalar_tensor_tensor(out=mo[o][:, :], in0=O[o][:, :], scalar=gsub[:, l:l + 1], in1=sq[:, :],
                                                                             op0=ALU.mult, op1=ALU.mult), reads=[BO[o], Bsq, B_small], writes=[Bmo[o]])
                                kb.dma("pool", mixT.ap()[2 + g, :, g0 + qc * 512:g0 + (qc + 1) * 512], mo[o][:, :], key=Bmo[o], reads=[Bmo[o]])
                            units.append((s_qk, s_mid, s_pv, s_post))
            _pipeline(units)
        kb.flush()


def phase_A3(nc, kb, sbt, pst, l, seqs, seq_off, qkT, vaug, mixT, bandm, B_const):
    SM = max(seqs)
    NBUF = 4
    with ExitStack() as st:
        qT = sbt(st, "A3_q", [128, SM], BF16); kT = sbt(st, "A3_k", [128, SM], BF16); Bq = Buf(); Bk = Buf()
        vts = [sbt(st, "A3_v%d" % i, [128, SM // 128, 2, 128], BF16) for i in range(2)]; Bvs = [Buf(), Buf()]
        ACC = sbt(st, "A3_acc", [128, 2, SM]); BACC = Buf()
        ps = [pst(st, "A3_ps%d" % i, [128, 512]) for i in range(NBUF)]; Bps = [Buf() for _ in range(NBUF)]
        pa = [pst(st, "A3_pa%d" % i, [128, 512]) for i in range(NBUF)]; Bpa = [Buf() for _ in range(NBUF)]
        ex = [sbt(st, "A3_ex%d" % i, [128, 3, 128], BF16) for i in range(NBUF)]; Bex = [Buf() for _ in range(NBUF)]
        pt = [sbt(st, "A3_pt%d" % i, [128, 3, 128], BF16) for i in range(NBUF)]; Bpt = [Buf() for _ in range(NBUF)]
        rr = [sbt(st, "A3_r%d" % i, [128, 1024]) for i in range(2)]; Brr = [Buf() for _ in range(2)]
        mo = [sbt(st, "A3_mo%d" % i, [128, 1024], BF16) for i in range(2)]; Bmo = [Buf() for _ in range(2)]
        groups = [(b, c, pi, d) for b in range(len(seqs)) for c in range(4) for pi, d in enumerate((1, 4, 16))]

        def load_v(G):
            b, c, pi, d = groups[G]
            S = seqs[b]; g0 = seq_off[b]; nm = S // (128 * d)
            vt = vts[G % 2]; Bv = Bvs[G % 2]
            for rho in range(d):
                src = bass.AP(vaug, (g0 + rho) * 2048 + (8 + 2 * c) * 128, [[d * 2048, 128], [128 * d * 2048, nm], [1, 256]])
                kb.dma("sp", vt[:, rho * nm:(rho + 1) * nm, :, :].rearrange("p n h d -> p n (h d)"), src, key=Bv, writes=[Bv])

        def load_qk(b, c):
            S = seqs[b]; g0 = seq_off[b]
            kb.dma("sp", qT[:, 0:S], qkT.ap()[8 + c, :, g0:g0 + S], key=Bq, writes=[Bq])
            kb.dma("sp", kT[:, 0:S], qkT.ap()[12 + c, :, g0:g0 + S], key=Bk, writes=[Bk])

        def finish(b, c, oi0):
            S = seqs[b]; g0 = seq_off[b]
            for n_, p0 in enumerate(range(0, S, 1024)):
                o = (oi0 + n_) % 2
                sl = slice(p0, p0 + 1024)
                kb.op("act", lambda e, o=o, sl=sl: e.activation(out=rr[o][0:64, :], in_=ACC[64:128, 0, sl], func=AF.Ln), reads=[BACC], writes=[Brr[o]])
                kb.op("act", lambda e, o=o, sl=sl: e.activation(out=rr[o][64:128, :], in_=ACC[0:64, 1, sl], func=AF.Ln), reads=[BACC], writes=[Brr[o]], own=False)
                kb.op("act", lambda e, o=o: e.activation(out=rr[o][:, :], in_=rr[o][:, :], func=AF.Exp, scale=-1.0), reads=[], writes=[Brr[o]])
                kb.op("pool", lambda e, o=o, sl=sl: e.tensor_tensor(out=mo[o][0:64, :], in0=ACC[0:64, 0, sl], in1=rr[o][0:64, :], op=ALU.mult),
                      reads=[BACC, Brr[o]], writes=[Bmo[o]])
                kb.op("pool", lambda e, o=o, sl=sl: e.tensor_tensor(out=mo[o][64:128, :], in0=ACC[64:128, 1, sl], in1=rr[o][64:128, :], op=ALU.mult),
                      reads=[BACC, Brr[o]], writes=[Bmo[o]])
                kb.dma("pool", mixT.ap()[4 + c, :, g0 + p0:g0 + p0 + 1024], mo[o][:, :], key=Bmo[o], reads=[Bmo[o]])

        units = []
        it = 0
        for G, (b, c, pi, d) in enumerate(groups):
            S = seqs[b]; nm = S // (128 * d)
            vt = vts[G % 2]; Bv = Bvs[G % 2]
            nun = d * nm * 2
            un = 0
            for rho in range(d):
                for m in range(nm):
                    olo = -1 if m > 0 else 0
                    ohi = 1 if m < nm - 1 else 0
                    qsl = slice(rho + d * 128 * m, rho + d * 128 * m + d * 127 + 1, d)
                    a, z = olo + 1, ohi + 2
                    for ee in range(2):
                        i = it % NBUF; it += 1
                        is_first = (un == 0)
                        un += 1
                        is_last = (un == nun)

                        def s_qk(i=i, m=m, olo=olo, ohi=ohi, qsl=qsl, rho=rho, d=d, is_first=is_first, b=b, c=c, pi=pi, ee=ee):
                            if is_first and pi == 0:
                                load_qk(b, c)

                            def qk(e):
                                for o in range(olo, ohi + 1):
                                    ksl = slice(rho + d * 128 * (m + o), rho + d * 128 * (m + o) + d * 127 + 1, d)
                                    mm = e.matmul(ps[i][:, (o + 1) * 128:(o + 2) * 128], lhsT=kT[64 * ee:64 * ee + 64, ksl],
                                                  rhs=qT[64 * ee:64 * ee + 64, qsl], start=True, stop=True)
                                return mm
                            kb.op("pe", qk, reads=[Bq, Bk], writes=[Bps[i]])

                        def s_exp(i=i, a=a, z=z):
                            kb.op("act", lambda e: e.activation(out=ex[i][:, a:z, :], in_=ps[i][:, a * 128:z * 128].rearrange("p (o q) -> p o q", q=128),
                                                                func=AF.Exp, scale=0.125), reads=[], writes=[Bps[i], Bex[i]])

                        def s_mask(i=i, a=a, z=z):
                            kb.op("dve", lambda e: e.tensor_tensor(out=pt[i][:, a:z, :], in0=ex[i][:, a:z, :], in1=bandm[:, a:z, :], op=ALU.mult),
                                  reads=[Bex[i], B_const], writes=[Bpt[i]])

                        def s_pv(i=i, m=m, olo=olo, ohi=ohi, rho=rho, nm=nm, vt=vt, Bv=Bv, ee=ee):
                            def pvm(e):
                                for o in range(olo, ohi + 1):
                                    mm = e.matmul(pa[i][:, 0:128], lhsT=vt[:, rho * nm + m + o, ee, :], rhs=pt[i][:, o + 1, :],
                                                  start=(o == olo), stop=(o == ohi))
                                return mm
                            kb.op("pe", pvm, reads=[Bv, Bpt[i]], writes=[Bpa[i]])

                        def s_post(i=i, qsl=qsl, pi=pi, is_last=is_last, b=b, c=c, G=G, is_first=is_first, ee=ee):
                            if is_first and G + 1 < len(groups):
                                load_v(G + 1)
                            if pi == 0:
                                kb.op("dve", lambda e: e.tensor_copy(out=ACC[:, ee, qsl], in_=pa[i][:, 0:128]), reads=[], writes=[Bpa[i], BACC])
                            else:
                                kb.op("dve", lambda e: e.tensor_tensor(out=ACC[:, ee, qsl], in0=pa[i][:, 0:128], in1=ACC[:, ee, qsl], op=ALU.add),
                                      reads=[], writes=[Bpa[i], BACC])
                            if is_last and pi == 2:
                                finish(b, c, G)
                        units.append((s_qk, s_exp, s_mask, s_pv, s_post))
        load_v(0)
        _pipeline(units)
        kb.flush()


def phase_F1(nc, kb, sbt, pst, l, seqs, seq_off, NB, x_src, mixT, wout_bf, gbc, lnbc, epst, B_const, xmid):
    with ExitStack() as st:
        w = sbt(st, "F1_w", [128, 8, D], BF16); Bw = Buf()
        kb.dma("sp", w[:], wout_bf.ap()[l, :, :, :], key=Bw, writes=[Bw])
        lg = sbt(st, "F1_lg", [128, D]); lb = sbt(st, "F1_lb", [128, D]); Bl = Buf()
        kb.dma("sp", lg[:], lnbc.ap()[l, 0, :, :], key=Bl, writes=[Bl])
        kb.dma("sp", lb[:], lnbc.ap()[l, 1, :, :], key=Bl, writes=[Bl])
        g1 = sbt(st, "F1_g1", [128, D]); Bg = Buf()
        mx = [sbt(st, "F1_mx%d" % i, [128, 8, 512], BF16) for i in range(2)]; Bmx = [Buf() for _ in range(2)]
        xt = [sbt(st, "F1_x%d" % i, [128, D]) for i in range(3)]; Bx = [Buf() for _ in range(3)]
        yt = [sbt(st, "F1_y%d" % i, [128, D]) for i in range(3)]; By = [Buf() for _ in range(3)]
        ot = [sbt(st, "F1_o%d" % i, [128, D]) for i in range(2)]; Bo = [Buf() for _ in range(2)]
        stt = [sbt(st, "F1_st%d" % i, [128, 32]) for i in range(2)]; Bst = [Buf() for _ in range(2)]
        po = [pst(st, "F1_po%d" % i, [128, 1024]) for i in range(2)]; Bpo = [Buf() for _ in range(2)]
        blk = 0; ti = 0
        for b in range(NB):
            kb.dma("sp", g1[:], gbc.ap()[l, b, 0, :, :], key=Bg, writes=[Bg])
            for t0 in range(0, seqs[b], 512):
                g0 = seq_off[b] + t0
                mb = blk % 2; blk += 1
                kb.dma("sp", mx[mb][:], mixT.ap()[:, :, g0:g0 + 512].rearrange("c p t -> p c t"), key=Bmx[mb], writes=[Bmx[mb]])
                for tt in range(4):
                    i3 = ti % 3; i2 = ti % 2; ti += 1
                    r0_ = g0 + tt * 128
                    kb.dma("sp", xt[i3][:], x_src.ap()[r0_:r0_ + 128, :], key=Bx[i3], writes=[Bx[i3]])

                    def mm(e, i2=i2, mb=mb, tt=tt):
                        for hf in range(2):
                            for fc in range(8):
                                m = e.matmul(po[i2][:, hf * 512:(hf + 1) * 512], lhsT=mx[mb][:, fc, tt * 128:(tt + 1) * 128],
                                             rhs=w[:, fc, hf * 512:(hf + 1) * 512], start=(fc == 0), stop=(fc == 7))
                        return m
                    kb.op("pe", mm, reads=[Bw, Bmx[mb]], writes=[Bpo[i2]])
                    kb.op("dve", lambda e, i3=i3, i2=i2: e.tensor_tensor(out=yt[i3][:, :], in0=po[i2][:, :], in1=g1[:, :], op=ALU.mult),
                          reads=[Bg], writes=[Bpo[i2], By[i3]])
                    kb.op("dve", lambda e, i3=i3: e.scalar_tensor_tensor(out=yt[i3][:, :], in0=xt[i3][:, :], scalar=ALPHA, in1=yt[i3][:, :], op0=ALU.mult, op1=ALU.add),
                          reads=[Bx[i3]], writes=[By[i3]])
                    _ln_tile(kb, yt[i3], By[i3], stt[i2], Bst[i2], epst, B_const, lg, lb, Bl, ot[i2], Bo[i2], "F1")
                    kb.dma("pool", xmid.ap()[r0_:r0_ + 128, :], ot[i2][:, :], key=Bo[i2], reads=[Bo[i2]])
        kb.flush()


def phase_F2(nc, kb, sbt, pst, l, seqs, seq_off, NB, xmid, wgu_bf, wdn_bf, gbc, lnbc, modT, B_modT, ident, epst, B_const, x_dst):
    with ExitStack() as st:
        wg = sbt(st, "F2_wg", [128, 8, 2 * NFF], BF16); Bwg = Buf()
        for kc in range(8):
            kb.dma("sp", wg[:, kc, :], wgu_bf.ap()[l, :, kc, :], key=Bwg, writes=[Bwg])
        wd = sbt(st, "F2_wd", [128, NFC, D], BF16); Bwd = Buf()
        kb.dma("sp", wd[:], wdn_bf.ap()[l, :, :, :], key=Bwd, writes=[Bwd])
        lg = sbt(st, "F2_lg", [128, D]); lb = sbt(st, "F2_lb", [128, D]); Bl = Buf()
        kb.dma("sp", lg[:], lnbc.ap()[l, 2, :, :], key=Bl, writes=[Bl])
        kb.dma("sp", lb[:], lnbc.ap()[l, 3, :, :], key=Bl, writes=[Bl])
        g2 = sbt(st, "F2_g2", [128, D]); Bg = Buf()
        xa = [sbt(st, "F2_xa%d" % i, [128, D]) for i in range(2)]; Bxa = [Buf() for _ in range(2)]
        xb = [sbt(st, "F2_xb%d" % i, [128, D]) for i in range(2)]; Bxb = [Buf() for _ in range(2)]
        hTs = [sbt(st, "F2_hT%d" % i, [128, 8, 512], BF16) for i in range(2)]; BhTs = [[Buf(), Buf()] for _ in range(2)]
        aT = sbt(st, "F2_aT", [128, NFC, 512], BF16); BaT = Buf()
        sl = [sbt(st, "F2_sl%d" % i, [128, 512], BF16) for i in range(2)]; Bsl = [Buf() for _ in range(2)]
        stt = [sbt(st, "F2_st%d" % i, [128, 32]) for i in range(2)]; Bst = [Buf() for _ in range(2)]
        pT = pst(st, "F2_pT", [128, 1024]); BpT = [Buf(), Buf()]
        pg = [pst(st, "F2_pg%d" % i, [128, 1024]) for i in range(2)]; Bpg = [Buf() for _ in range(2)]
        po = pst(st, "F2_po", [128, 1024]); Bpo = Buf()
        blocks = [(b, t0) for b in range(NB) for t0 in range(0, seqs[b], 512)]
        cnt = {"xi": 0, "gi": 0, "yi": 0}

        def prep_tile(bi, tt):
            b, t0 = blocks[bi]
            g0 = seq_off[b] + t0
            hT = hTs[bi % 2]; BhT = BhTs[bi % 2]
            i = cnt["xi"] % 2; cnt["xi"] += 1
            kb.dma("sp", xa[i][:], xmid.ap()[g0 + tt * 128:g0 + (tt + 1) * 128, :], key=Bxa[i], writes=[Bxa[i]])

            def tr(e, i=i):
                for kc in range(8):
                    m = e.transpose(pT[:, kc * 128:(kc + 1) * 128], xa[i][:, kc * 128:(kc + 1) * 128], ident[:])
                return m
            kb.op("pe", tr, reads=[Bxa[i], B_const], writes=[BpT[0], BpT[1]])
            for kc in range(8):
                if kc < 4:
                    kb.op("act", lambda e, kc=kc: e.activation(
                        out=hT[:, kc, tt * 128:(tt + 1) * 128], in_=pT[:, kc * 128:(kc + 1) * 128], func=AF.Identity,
                        scale=modT[:, l, 3, kc, b:b + 1], bias=modT[:, l, 2, kc, b:b + 1]), reads=[B_modT], writes=[BpT[0], BhT[0]], own=False)
                else:
                    kb.op("dve", lambda e, kc=kc: e.tensor_scalar(
                        out=hT[:, kc, tt * 128:(tt + 1) * 128], in0=pT[:, kc * 128:(kc + 1) * 128],
                        scalar1=modT[:, l, 3, kc, b:b + 1], scalar2=modT[:, l, 2, kc, b:b + 1], op0=ALU.mult, op1=ALU.add),
                        reads=[B_modT], writes=[BpT[1], BhT[1]], own=False)

        for tt in range(4):
            prep_tile(0, tt)
        cur_b = -1
        for bi, (b, t0) in enumerate(blocks):
            g0 = seq_off[b] + t0
            hT = hTs[bi % 2]; BhT = BhTs[bi % 2]
            if b != cur_b:
                cur_b = b
                kb.dma("sp", g2[:], gbc.ap()[l, b, 1, :, :], key=Bg, writes=[Bg])
            for fc in range(NFC):
                i = cnt["gi"] % 2; cnt["gi"] += 1

                def gu(e, i=i, fc=fc, hT=hT):
                    for part in range(2):
                        col = part * NFF + fc * 128
                        for kc in range(8):
                            m = e.matmul(pg[i][:, part * 512:(part + 1) * 512], lhsT=wg[:, kc, col:col + 128], rhs=hT[:, kc, :],
                                         start=(kc == 0), stop=(kc == 7))
                    return m
                kb.op("pe", gu, reads=[Bwg] + BhT, writes=[Bpg[i]])
                kb.op("act", lambda e, i=i: e.activation(out=sl[i][:, :], in_=pg[i][:, 0:512], func=AF.Silu), reads=[], writes=[Bpg[i], Bsl[i]])
                kb.op("dve", lambda e, i=i, fc=fc: e.tensor_tensor(out=aT[:, fc, :], in0=pg[i][:, 512:1024], in1=sl[i][:, :], op=ALU.mult),
                      reads=[Bsl[i]], writes=[Bpg[i], BaT])
                if bi + 1 < len(blocks) and fc in (4, 9, 14, 19):
                    prep_tile(bi + 1, (fc - 4) // 5)
            for tt in range(4):
                i = cnt["yi"] % 2; cnt["yi"] += 1
                r0_ = g0 + tt * 128
                kb.dma("sp", xb[i][:], xmid.ap()[r0_:r0_ + 128, :], key=Bxb[i], writes=[Bxb[i]])

                def dn(e, tt=tt):
                    for hf in range(2):
                        for fc in range(NFC):
                            m = e.matmul(po[:, hf * 512:(hf + 1) * 512], lhsT=aT[:, fc, tt * 128:(tt + 1) * 128],
                                         rhs=wd[:, fc, hf * 512:(hf + 1) * 512], start=(fc == 0), stop=(fc == NFC - 1))
                    return m
                kb.op("pe", dn, reads=[Bwd, BaT], writes=[Bpo])
                kb.op("dve", lambda e, i=i: e.tensor_scalar(out=xb[i][:, :], in0=xb[i][:, :], scalar1=ALPHA, scalar2=None, op0=ALU.mult),
                      reads=[], writes=[Bxb[i]])
                kb.op("dve", lambda e: e.tensor_tensor(out=po[:, :], in0=po[:, :], in1=g2[:, :], op=ALU.mult), reads=[Bg], writes=[Bpo])
                kb.op("dve", lambda e, i=i: e.tensor_tensor(out=xb[i][:, :], in0=po[:, :], in1=xb[i][:, :], op=ALU.add), reads=[], writes=[Bpo, Bxb[i]])
                _ln_tile(kb, xb[i], Bxb[i], stt[i], Bst[i], epst, B_const, lg, lb, Bl, xb[i], Bxb[i], "F2")
                kb.dma("pool", x_dst.ap()[r0_:r0_ + 128, :], xb[i][:, :], key=Bxb[i], reads=[Bxb[i]])
        kb.flush()


_PERM = _win_perm()


def _core_inputs(xc, cc, shared):
    NB = cc.shape[0]
    cT = np.ascontiguousarray(cc.reshape(NB, 8, 128).transpose(2, 1, 0)).astype(np.float32)
    d = {"x": np.ascontiguousarray(xc), "cT": cT}
    d.update(shared)
    return d


def _shared_inputs(w_ada, b_ada, w_in, na_rpb, diff_lambda, diff_subln_g, w_out, ln1_g, ln1_b, w_gu, w_down, ln2_g, ln2_b, smax):
    L = w_ada.shape[0]
    ident, colmask2, band, bd, perms = _consts()
    f = lambda a: np.ascontiguousarray(np.asarray(a, dtype=np.float32))
    return {
        "w_ada": f(w_ada), "b_ada": f(b_ada), "w_in": f(np.asarray(w_in)[:, :, _PERM]),
        "rpb": f(np.asarray(na_rpb).reshape(L, 60, 31)), "lam": f(np.asarray(diff_lambda).reshape(L, 128)),
        "subg": f(np.tile(np.asarray(diff_subln_g), (1, 2))), "w_out": f(w_out), "w_gu": f(w_gu), "w_down": f(w_down),
        "lnp": f(np.stack([ln1_g, ln1_b, ln2_g, ln2_b], 1)), "rope": _rope_tables(smax),
        "ident": ident, "colmask": colmask2, "band": band, "bd": bd, "perms": perms,
    }


def kernel(x_prompt, x_sample, c_prompt, c_sample, w_ada, b_ada, w_in, na_rpb, diff_lambda, diff_subln_g, w_out,
           ln1_g, ln1_b, w_gu, w_down, ln2_g, ln2_b):
    x_prompt = np.asarray(x_prompt); x_sample = np.asarray(x_sample)
    c_prompt = np.asarray(c_prompt); c_sample = np.asarray(c_sample)
    NCORE = 8
    Bp, Sp, _ = x_prompt.shape; Bs, Ss, _ = x_sample.shape
    per = Bs // NCORE
    seqs = [Sp] + [Ss] * per
    depth = np.asarray(w_ada).shape[0]
    nc, kb = build(seqs, depth)
    shared = _shared_inputs(w_ada, b_ada, w_in, na_rpb, diff_lambda, diff_subln_g, w_out, ln1_g, ln1_b, w_gu, w_down, ln2_g, ln2_b, max(seqs))
    in_maps = []
    for c in range(NCORE):
        xc = np.concatenate([x_prompt[c]] + [x_sample[c * per + j] for j in range(per)], 0)
        cc = np.concatenate([c_prompt[c:c + 1], c_sample[c * per:(c + 1) * per]], 0)
        in_maps.append(_core_inputs(xc, cc, shared))
    res = run_bass_kernel_spmd(nc, in_maps, core_ids=list(range(NCORE)))
    yp = np.empty((Bp, Sp, D), np.float32); ys = np.empty((Bs, Ss, D), np.float32)
    for c in range(NCORE):
        y = res.results[c]["y"]
        yp[c] = y[0:Sp]
        for j in range(per):
            ys[c * per + j] = y[Sp + j * Ss:Sp + (j + 1) * Ss]
    return (yp, ys)
```

```python
import math
import numpy as np
from contextlib import ExitStack
import concourse.bass as bass
import concourse.mybir as mybir
from concourse.bass_utils import run_bass_kernel_spmd

F32 = mybir.dt.float32
BF16 = mybir.dt.bfloat16
AF = mybir.ActivationFunctionType
ALU = mybir.AluOpType

D = 1024
NFF = 2816
NFC = NFF // 128
WIN = 3072
ALPHA = 4.0 ** 0.25
EPS = 1e-5
THETA = 10000.0
import os as _os
MASK_ENG = _os.environ.get("MASK_ENG", "dve")


class DKey:
    __slots__ = ("dsem", "dcnt", "phase", "queue")

    def __init__(self, queue):
        self.dsem = None
        self.dcnt = 0
        self.phase = -1
        self.queue = queue


class Buf:
    __slots__ = ("name", "w", "r", "kq")

    def __init__(self, name=""):
        self.name = name
        self.w = {}
        self.r = {}
        self.kq = {}


class KB:
    ENG = ("pe", "act", "dve", "pool", "sp")

    def __init__(self, nc, stack):
        self.nc = nc
        self.stack = stack
        self.q = {e: [] for e in self.ENG}
        self.cnt = {e: 0 for e in self.ENG}
        self.seen = {e: {} for e in self.ENG}
        self.sems = {e: stack.enter_context(nc.semaphore("s_" + e)) for e in self.ENG}
        self.ndsem = 0
        self.dma_bufs = []
        self.ninstr = 0
        self.phase = 0
        self.free_dsems = {"sp": [], "pool": []}

    def _gather(self, eng, reads, writes, strict_own=False, own=True):
        need = {}
        for b in reads:
            for k, v in b.w.items():
                if need.get(k, 0) < v:
                    need[k] = v
        for b in writes:
            for k, v in b.w.items():
                if need.get(k, 0) < v:
                    need[k] = v
            for k, v in b.r.items():
                if need.get(k, 0) < v:
                    need[k] = v
        waits = []
        seen = self.seen[eng]
        for k, v in need.items():
            if not isinstance(k, str) and k.phase != self.phase:
                continue
            if k == eng and (not own or v > self.cnt[eng] or (not strict_own and (eng in ("pe", "sp") or v < self.cnt[eng] - 1))):
                continue
            if seen.get(k, 0) >= v:
                continue
            seen[k] = v
            waits.append((k, v))
        return waits

    def _semof(self, k):
        return self.sems[k] if isinstance(k, str) else k.dsem

    def op(self, eng, fn, reads=(), writes=(), inc=True, own=True):
        waits = self._gather(eng, reads, writes, own=own)
        semo = self.sems[eng]
        wl = [(self._semof(k), v) for k, v in waits]
        if inc:
            self.cnt[eng] += 1
            val = self.cnt[eng]
        else:
            val = self.cnt[eng] + 1

        def emit(e, fn=fn, wl=wl, inc=inc, semo=semo):
            for s, v in wl:
                e.wait_ge(s, v)
            ins = fn(e)
            if inc:
                ins.then_inc(semo, 1)
        self.q[eng].append(emit)
        self.ninstr += 1 + len(wl)
        for b in writes:
            b.w = {eng: val}
            b.r = {}
        for b in reads:
            if b.r.get(eng, 0) < val:
                b.r[eng] = val
        return val

    def dma(self, queue, out, in_, key, reads=(), writes=(), **kw):
        buf = key
        key = buf.kq.get(queue)
        if key is None:
            key = buf.kq[queue] = DKey(queue)
        if key.dsem is None or key.phase != self.phase:
            if self.free_dsems[queue]:
                key.dsem, key.dcnt = self.free_dsems[queue].pop()
            else:
                key.dsem = self.stack.enter_context(self.nc.semaphore("d%s%d" % (queue[0], self.ndsem)))
                key.dcnt = 0
                self.ndsem += 1
            key.phase = self.phase
            self.dma_bufs.append(key)
        waits = self._gather(queue, reads, writes, strict_own=True)
        wl = [(self._semof(k), v) for k, v in waits]
        key.dcnt += 16
        val = key.dcnt
        sem = key.dsem

        def emit(e, out=out, in_=in_, wl=wl, sem=sem, kw=kw):
            for s, v in wl:
                e.wait_ge(s, v)
            e.dma_start(out=out, in_=in_, **kw).then_inc(sem, 16)
        self.q[queue].append(emit)
        self.ninstr += 1 + len(wl)
        for b in writes:
            b.w = {key: val}
            b.r = {}
        for b in reads:
            if b.r.get(key, 0) < val:
                b.r[key] = val
        return val

    def flush(self):
        nc = self.nc
        fin = {e: [] for e in self.ENG}
        for b in self.dma_bufs:
            if b.dcnt and self.seen["sp"].get(b, 0) < b.dcnt:
                fin["sp"].append((b.dsem, b.dcnt))
                self.seen["sp"][b] = b.dcnt
        self.cnt["sp"] += 1
        spv = self.cnt["sp"]
        sps = self.sems["sp"]
        for e in self.ENG:
            for o in self.ENG:
                if o != e and self.cnt[o] and self.seen[e].get(o, 0) < self.cnt[o]:
                    fin[e].append((self.sems[o], self.cnt[o]))
                    self.seen[e][o] = self.cnt[o]
        q = self.q

        def mk(e):
            def run(eng):
                for f in q[e]:
                    f(eng)
                if e == "sp":
                    for s, v in fin[e]:
                        if s is not sps:
                            eng.wait_ge(s, v)
                    eng.nop().then_inc(sps, 1)
                else:
                    for s, v in fin[e]:
                        eng.wait_ge(s, v)
            return run
        with nc.Block() as block:
            block.tensor(mk("pe"))
            block.scalar(mk("act"))
            block.vector(mk("dve"))
            block.gpsimd(mk("pool"))
            block.sync(mk("sp"))
        self.q = {e: [] for e in self.ENG}
        for b in self.dma_bufs:
            self.free_dsems[b.queue].append((b.dsem, b.dcnt))
        dead = set(id(b) for b in self.dma_bufs)
        self._dead = getattr(self, "_dead", set()) | dead
        for e in self.ENG:
            self.seen[e] = {k: v for k, v in self.seen[e].items() if isinstance(k, str)}
        self.dma_bufs = []
        self.phase += 1


def _win_perm():
    qa = list(range(0, 256)); ka = list(range(256, 512)); va = list(range(512, 768))
    qb = list(range(768, 1024)); kb_ = list(range(1024, 1280)); vb = list(range(1280, 1536))
    qc = list(range(1536, 2048)); kc = list(range(2048, 2560)); vc = list(range(2560, 3072))
    A = qb + kb_ + qc + kc

    def swap(cols, hd):
        out = []
        for i in range(0, len(cols), hd):
            blk = cols[i:i + hd]
            out += blk[hd // 2:] + blk[:hd // 2]
        return out
    Bc = swap(qb, 32) + swap(kb_, 32) + swap(qc, 64) + swap(kc, 64)
    return np.array(qa + ka + A + va + vb + vc, dtype=np.int64)


def _rope_tables(smax):
    f32 = np.float32
    t = np.arange(smax, dtype=f32)

    def tab(half, reps):
        inv = (f32(THETA) ** (-(np.arange(half, dtype=f32) / f32(half)))).astype(f32)
        ang = (t[None, :] * inv[:, None]).astype(f32)
        c = np.cos(ang).astype(f32); s = np.sin(ang).astype(f32)
        cos = np.concatenate([c, c], 0)
        sins = np.concatenate([-s, s], 0)
        return np.tile(cos, (reps, 1)), np.tile(sins, (reps, 1))
    cb, sb = tab(16, 4)
    cc, sc = tab(32, 2)
    return np.ascontiguousarray(np.stack([cb, sb, cc, sc], 0))


def _consts():
    ident = np.eye(128, dtype=np.float32)
    cq = np.arange(64)
    cs = np.clip(cq - 8, 0, 48)
    colin = (cq[None, :] >= cs[:, None]) & (cq[None, :] < cs[:, None] + 16)
    colmaskT = colin.T.astype(np.float32)
    colmask2 = np.concatenate([colmaskT, colmaskT], 0)
    k = np.arange(128)[:, None]; q = np.arange(128)[None, :]
    band = np.stack([(np.abs(128 * o + k - q) <= 64).astype(np.float32) for o in (-1, 0, 1)], 1)
    bd = np.zeros((128, 128), np.float32); bd[:64, :64] = 1; bd[64:, 64:] = 1
    perms = np.zeros((2, 128, 128), np.float32)
    for t_, hd in enumerate((32, 64)):
        for m in range(128):
            blk, r = divmod(m, hd)
            perms[t_, blk * hd + (r + hd // 2) % hd, m] = 1.0
    return ident, colmask2, np.ascontiguousarray(band), bd, perms


def build(seqs, depth, dbg=False, phases=None):
    NB = len(seqs)
    T = sum(seqs)
    SMAX = max(seqs)
    seq_off = [sum(seqs[:i]) for i in range(NB)]
    nc = bass.Bass("TRN2", target_bir_lowering=False)

    def din(name, shape, dt=F32):
        return nc.dram_tensor(name, list(shape), dt, kind="ExternalInput")

    def dscr(name, shape, dt):
        if dbg and name in ("qkT", "vaug", "mixT", "xmid", "zz_d", "gbc", "lnbc"):
            return nc.dram_tensor(name, list(shape), dt, kind="ExternalOutput")
        return nc.dram_tensor(name, list(shape), dt)

    x_in = din("x", [T, D]); cT_in = din("cT", [128, 8, NB])
    wada_in = din("w_ada", [depth, D, 6 * D]); bada_in = din("b_ada", [depth, 6 * D])
    win_in = din("w_in", [depth, D, WIN]); rpb_in = din("rpb", [depth, 60, 31])
    lam_in = din("lam", [depth, 128]); subg_in = din("subg", [depth, 128])
    wout_in = din("w_out", [depth, D, D]); wgu_in = din("w_gu", [depth, D, 2 * NFF]); wdn_in = din("w_down", [depth, NFF, D])
    lnp_in = din("lnp", [depth, 4, D])
    rope_in = din("rope", [4, 128, SMAX]); ident_in = din("ident", [128, 128]); colmask_in = din("colmask", [128, 64])
    band_in = din("band", [128, 3, 128]); bd_in = din("bd", [128, 128]); perms_in = din("perms", [2, 128, 128])
    y_out = nc.dram_tensor("y", [T, D], F32, kind="ExternalOutput")

    win_bf = dscr("win_bf", [depth, 128, 8, WIN], BF16)
    wout_bf = dscr("wout_bf", [depth, 128, 8, D], BF16)
    wgu_bf = dscr("wgu_bf", [depth, 128, 8, 2 * NFF], BF16)
    wdn_bf = dscr("wdn_bf", [depth, 128, NFC, D], BF16)
    gbc = dscr("gbc", [depth, NB, 2, 128, D], F32)
    lnbc = dscr("lnbc", [depth, 4, 128, D], F32)
    gpad = dscr("gpad", [depth, 61, 127], F32)
    zz_d = dscr("zz_d", [depth, 128, 56 * 64], BF16)
    qkT = dscr("qkT", [16, 128, T], BF16)
    vaug = dscr("vaug", [T, 16, 128], BF16)
    mixT = dscr("mixT", [8, 128, T], BF16)
    xmid = dscr("xmid", [T, D], F32)
    xlay = dscr("xlay", [T, D], F32)

    dbg_outs = {}
    with ExitStack() as top:
        kb = KB(nc, top)

        uid = [0]

        def sbt(st, name, shape, dt=F32):
            uid[0] += 1
            return st.enter_context(nc.sbuf_tensor("sb%d_%s" % (uid[0], name), list(shape), dt))

        def pst(st, name, shape, dt=F32):
            uid[0] += 1
            return st.enter_context(nc.psum_tensor("ps%d_%s" % (uid[0], name), list(shape), dt))

        ident = sbt(top, "ident", [128, 128]); identb = sbt(top, "identb", [128, 128], BF16)
        ones_row = sbt(top, "ones_row", [1, 128]); bdones = sbt(top, "bdones", [128, 128]); permt = sbt(top, "permt", [128, 2, 128])
        modT = sbt(top, "modT", [128, depth, 4, 8, NB])
        neglam = sbt(top, "neglam", [128, depth]); gsub = sbt(top, "gsub", [128, depth])
        epst = sbt(top, "epst", [128, 1])
        bandm = sbt(top, "bandm", [128, 3, 128], BF16)
        B_const = Buf("const")
        B_modT = Buf("modT"); B_small = Buf("small")

        with ExitStack() as st:
            kb.dma("sp", ident[:], ident_in.ap()[:, :], key=B_const, writes=[B_const])
            kb.dma("sp", bdones[:], bd_in.ap()[:, :], key=B_const, writes=[B_const])
            kb.dma("sp", permt[:], perms_in.ap().rearrange("t k m -> k t m"), key=B_const, writes=[B_const])
            kb.op("dve", lambda e: e.tensor_copy(out=identb[:], in_=ident[:]), reads=[B_const], writes=[B_const])
            kb.op("dve", lambda e: e.memset(ones_row[:], 1.0), writes=[B_const])
            kb.op("dve", lambda e: e.memset(epst[:], EPS), writes=[B_const])
            bandf = sbt(st, "bandf", [128, 3, 128]); B_bandf = Buf()
            kb.dma("sp", bandf[:], band_in.ap()[:, :, :], key=B_bandf, writes=[B_bandf])
            kb.op("dve", lambda e: e.tensor_copy(out=bandm[:], in_=bandf[:]), reads=[B_bandf], writes=[B_const])

            stg = [sbt(st, "stg%d" % i, [128, 5632]) for i in range(2)]
            stb = [sbt(st, "stb%d" % i, [128, 5632], BF16) for i in range(2)]
            Bstg = [Buf() for _ in range(2)]; Bstb = [Buf() for _ in range(2)]
            cnt = [0]

            def conv(src_ap, dst_ap, ncols):
                i = cnt[0] % 2; cnt[0] += 1
                kb.dma("sp", stg[i][:, 0:ncols], src_ap, key=Bstg[i], writes=[Bstg[i]])
                eng = ("dve", "pool", "act")[cnt[0] % 3]
                if eng == "act":
                    kb.op("act", lambda e: e.copy(out=stb[i][:, 0:ncols], in_=stg[i][:, 0:ncols]), reads=[Bstg[i]], writes=[Bstb[i]])
                else:
                    kb.op(eng, lambda e: e.tensor_copy(out=stb[i][:, 0:ncols], in_=stg[i][:, 0:ncols]), reads=[Bstg[i]], writes=[Bstb[i]])
                kb.dma("pool", dst_ap, stb[i][:, 0:ncols], key=Bstb[i], reads=[Bstb[i]])
            for l in range(depth):
                for kc in range(8):
                    conv(win_in.ap()[l, kc * 128:(kc + 1) * 128, :], win_bf.ap()[l, :, kc, :], WIN)
                    conv(wgu_in.ap()[l, kc * 128:(kc + 1) * 128, :], wgu_bf.ap()[l, :, kc, :], 2 * NFF)
                    conv(wout_in.ap()[l, kc * 128:(kc + 1) * 128, :], wout_bf.ap()[l, :, kc, :], D)
                for fc in range(NFC):
                    conv(wdn_in.ap()[l, fc * 128:(fc + 1) * 128, :], wdn_bf.ap()[l, :, fc, :], D)

            kb.flush()
        with ExitStack() as st:
            cT = sbt(st, "cT", [128, 8, NB]); siluT = sbt(st, "siluT", [128, 8, NB]); B_c = Buf()
            silubc = sbt(st, "silubc", [128, NB, 8, 128])
            kb.dma("sp", cT[:], cT_in.ap()[:, :, :], key=B_c, writes=[B_c])
            kb.op("act", lambda e: e.activation(out=siluT[:], in_=cT[:], func=AF.Silu), reads=[B_c], writes=[B_c])
            for b in range(NB):
                kb.op("dve", lambda e, b=b: e.tensor_copy(out=silubc[:, b, :, :], in_=siluT[:, :, b:b + 1].broadcast_to([128, 8, 128])),
                      reads=[B_c], writes=[B_c])
            ones_nb = sbt(st, "ones_nb", [1, 128])
            kb.op("dve", lambda e: e.memset(ones_nb[:], 1.0), writes=[B_c])
            wa = [sbt(st, "wa%d" % i, [128, 8, 1024]) for i in range(2)]; Bwa = [Buf() for _ in range(2)]
            brow = [sbt(st, "brow%d" % i, [1, 1024]) for i in range(2)]; Bbrow = [Buf() for _ in range(2)]
            pm = [pst(st, "pm%d" % i, [128, 512]) for i in range(2)]; Bpm = [Buf() for _ in range(2)]
            gst = [sbt(st, "gst%d" % i, [128, 1024]) for i in range(2)]; Bgst = [Buf() for _ in range(2)]
            it = 0
            for l in range(depth):
                for kind in range(6):
                    i = it % 2; it += 1
                    kb.dma("sp", wa[i][:], wada_in.ap()[l, :, kind * 1024:(kind + 1) * 1024].rearrange("(kc p) n -> p kc n", p=128),
                           key=Bwa[i], writes=[Bwa[i]])
                    kb.dma("sp", brow[i][:], bada_in.ap()[l:l + 1, kind * 1024:(kind + 1) * 1024], key=Bbrow[i], writes=[Bbrow[i]])
                    if kind in (0, 1, 3, 4):
                        slot = {0: 0, 1: 1, 3: 2, 4: 3}[kind]
                        pmt = pm[0]

                        def mm_fm(e, i=i, pmt=pmt):
                            for ncn in range(8):
                                for kc in range(8):
                                    e.matmul(pmt[:, ncn * NB:(ncn + 1) * NB], lhsT=wa[i][:, kc, ncn * 128:(ncn + 1) * 128],
                                             rhs=siluT[:, kc, :], start=(kc == 0), stop=False)
                                m = e.matmul(pmt[:, ncn * NB:(ncn + 1) * NB], lhsT=brow[i][0:1, ncn * 128:(ncn + 1) * 128],
                                             rhs=ones_nb[0:1, 0:NB], start=False, stop=True)
                            return m
                        kb.op("pe", mm_fm, reads=[Bwa[i], Bbrow[i], B_c], writes=[Bpm[0]])
                        addv = 1.0 if kind in (1, 4) else 0.0
                        kb.op("dve", lambda e, l=l, slot=slot, addv=addv, pmt=pmt: e.tensor_scalar(
                            out=modT[:, l, slot, :, :], in0=pmt[:, 0:8 * NB].rearrange("p (a b) -> p a b", b=NB),
                            scalar1=addv, scalar2=None, op0=ALU.add), reads=[Bpm[0]], writes=[Bpm[0], B_modT])
                    else:
                        which = 0 if kind == 2 else 1
                        for b in range(NB):
                            gi = (b + which) % 2
                            for hf in range(2):
                                def mm_bc(e, i=i, b=b, hf=hf):
                                    for kc in range(8):
                                        e.matmul(pm[1][:, :], lhsT=silubc[:, b, kc, :], rhs=wa[i][:, kc, hf * 512:(hf + 1) * 512],
                                                 start=(kc == 0), stop=False)
                                    return e.matmul(pm[1][:, :], lhsT=ones_row[0:1, :], rhs=brow[i][0:1, hf * 512:(hf + 1) * 512],
                                                    start=False, stop=True)
                                kb.op("pe", mm_bc, reads=[Bwa[i], Bbrow[i], B_c, B_const], writes=[Bpm[1]])
                                kb.op("act", lambda e, gi=gi, hf=hf: e.copy(out=gst[gi][:, hf * 512:(hf + 1) * 512], in_=pm[1][:, :]),
                                      reads=[Bpm[1]], writes=[Bpm[1], Bgst[gi]])
                            kb.dma("pool", gbc.ap()[l, b, which, :, :], gst[gi][:], key=Bgst[gi], reads=[Bgst[gi]])
            kb.flush()
        with ExitStack() as st:
            pm = [pst(st, "pmb%d" % i, [128, 512]) for i in range(2)]; Bpm = [Buf() for _ in range(2)]
            gst = [sbt(st, "gstb%d" % i, [128, 1024]) for i in range(2)]; Bgst = [Buf() for _ in range(2)]
            lrow = sbt(st, "lrow", [1, depth * 4 * D]); B_lrow = Buf()
            kb.dma("sp", lrow[:], lnp_in.ap().rearrange("l k d -> (l k d)").rearrange("(o n) -> o n", o=1), key=B_lrow, writes=[B_lrow])
            for l in range(depth):
                for k4 in range(4):
                    gi = (l * 4 + k4) % 2
                    for hf in range(2):
                        off = (l * 4 + k4) * D + hf * 512
                        kb.op("pe", lambda e, off=off: e.matmul(pm[1][:, :], lhsT=ones_row[0:1, :], rhs=lrow[0:1, off:off + 512],
                                                                 start=True, stop=True), reads=[B_lrow, B_const], writes=[Bpm[1]])
                        kb.op("act", lambda e, gi=gi, hf=hf: e.copy(out=gst[gi][:, hf * 512:(hf + 1) * 512], in_=pm[1][:, :]),
                              reads=[Bpm[1]], writes=[Bpm[1], Bgst[gi]])
                    kb.dma("pool", lnbc.ap()[l, k4, :, :], gst[gi][:], key=Bgst[gi], reads=[Bgst[gi]])
            lamr = sbt(st, "lamr", [1, depth, 128]); lwk = sbt(st, "lwk", [1, depth, 8]); B_lam = Buf()
            kb.dma("sp", lamr[:], lam_in.ap().rearrange("(o l) n -> o l n", o=1), key=B_lam, writes=[B_lam])
            sg = sbt(st, "sg", [128, depth]); B_sg = Buf()
            kb.dma("sp", sg[:], subg_in.ap().rearrange("l p -> p l"), key=B_sg, writes=[B_sg], allow_slow_non_contiguous=True)
            for l in range(depth):
                linit = 0.8 - 0.6 * math.exp(-0.3 * l)
                prod = sbt(st, "prod%d" % l, [1, 64])
                kb.op("dve", lambda e, l=l, prod=prod: e.tensor_tensor(
                    out=prod[:].rearrange("o (a b) -> o a b", a=2), in0=lamr[0:1, l, :].rearrange("o (a c b) -> o a c b", a=2, c=2)[:, :, 0, :],
                    in1=lamr[0:1, l, :].rearrange("o (a c b) -> o a c b", a=2, c=2)[:, :, 1, :], op=ALU.mult), reads=[B_lam], writes=[B_lam])
                kb.op("dve", lambda e, l=l, prod=prod: e.reduce_sum(out=lwk[0:1, l, 0:2], in_=prod[:].rearrange("o (a b) -> o a b", a=2),
                                                                    axis=mybir.AxisListType.X), reads=[B_lam], writes=[B_lam])
                kb.op("act", lambda e, l=l: e.activation(out=lwk[0:1, l, 2:4], in_=lwk[0:1, l, 0:2], func=AF.Exp), reads=[B_lam], writes=[B_lam])
                kb.op("dve", lambda e, l=l: e.tensor_tensor(out=lwk[0:1, l, 4:5], in0=lwk[0:1, l, 3:4], in1=lwk[0:1, l, 2:3], op=ALU.subtract),
                      reads=[B_lam], writes=[B_lam])
                kb.op("dve", lambda e, l=l, linit=linit: e.tensor_scalar(out=lwk[0:1, l, 5:6], in0=lwk[0:1, l, 4:5], scalar1=-linit, scalar2=None, op0=ALU.add),
                      reads=[B_lam], writes=[B_lam])
                kb.op("pe", lambda e, l=l: e.matmul(pm[1][:, 0:1], lhsT=ones_row[0:1, :], rhs=lwk[0:1, l, 5:6], start=True, stop=True),
                      reads=[B_lam, B_const], writes=[Bpm[1]])
                kb.op("act", lambda e, l=l: e.copy(out=neglam[:, l:l + 1], in_=pm[1][:, 0:1]), reads=[Bpm[1]], writes=[Bpm[1], B_small])
                kb.op("act", lambda e, l=l, linit=linit: e.mul(out=gsub[:, l:l + 1], in_=sg[:, l:l + 1], mul=(1.0 - linit)), reads=[B_sg], writes=[B_small])
            kb.flush()
        with ExitStack() as st:
            ZZ = sbt(st, "ZZ", [128, depth, 4, 14, 64], BF16); B_ZZ = Buf()
            rp = sbt(st, "rp", [61, depth, 31]); gp = sbt(st, "gp", [61, depth, 127]); B_rp = Buf(); B_gpad = Buf()
            hk = sbt(st, "hk", [128, depth, 4, 14, 64]); cmask = sbt(st, "cmask", [128, 64]); B_hk = Buf()
            kb.dma("sp", cmask[:], colmask_in.ap()[:, :], key=B_hk, writes=[B_hk])
            kb.op("pool", lambda e: e.memset(gp[:], 0.0), writes=[B_rp])
            kb.op("pool", lambda e: e.memset(rp[:], 0.0), writes=[B_rp])
            for l in range(depth):
                kb.dma("sp", rp[0:60, l, :], rpb_in.ap()[l, :, :], key=B_rp, writes=[B_rp])
            kb.op("act", lambda e: e.activation(out=gp[0:60, :, 48:79], in_=rp[0:60, :, :], func=AF.Exp), reads=[B_rp], writes=[B_rp])
            for l in range(depth):
                kb.dma("pool", gpad.ap()[l, :, :], gp[:, l, :], key=B_rp, reads=[B_rp], writes=[B_gpad])
            for l in range(depth):
                for h in range(4):
                    for half in range(2):
                        src = bass.AP(gpad, (l * 61 + h * 15 + half) * 127, [[1, 64], [127, 14], [1, 64]])
                        kb.dma("sp", hk[half * 64:(half + 1) * 64, l, h, :, :], src, key=B_hk, reads=[B_gpad], writes=[B_hk])
            for l in range(depth):
                kb.op("dve", lambda e, l=l: e.tensor_tensor(
                    out=ZZ[:, l, :, :, :].rearrange("p h d c -> p (h d) c"), in0=hk[:, l, :, :, :].rearrange("p h d c -> p (h d) c")[:, :, ::-1],
                    in1=cmask[:, :].unsqueeze(1).broadcast_to([128, 56, 64]), op=ALU.mult), reads=[B_hk], writes=[B_ZZ])
                kb.dma("pool", zz_d.ap()[l, :, :], ZZ[:, l, :, :, :].rearrange("p h d c -> p (h d c)"), key=B_ZZ, reads=[B_ZZ])
            kb.flush()

        for l in range(depth):
            x_src = x_in if l == 0 else xlay
            x_dst = y_out if l == depth - 1 else xlay
            ph = (lambda n: phases is None or n in phases)
            if ph("P"):
                phase_P(nc, kb, sbt, pst, l, seqs, seq_off, NB, x_src, win_bf, rope_in, modT, B_modT, ident, B_const, qkT, vaug, permt)
            if ph("A1"):
                phase_A1(nc, kb, sbt, pst, l, seqs, seq_off, qkT, vaug, mixT, zz_d)
            if ph("A2"):
                phase_A2(nc, kb, sbt, pst, l, seqs, seq_off, qkT, vaug, mixT, neglam, gsub, epst, bdones, B_small, B_const)
            if ph("A3"):
                phase_A3(nc, kb, sbt, pst, l, seqs, seq_off, qkT, vaug, mixT, bandm, B_const)
            if ph("F1"):
                phase_F1(nc, kb, sbt, pst, l, seqs, seq_off, NB, x_src, mixT, wout_bf, gbc, lnbc, epst, B_const, xmid)
            if ph("F2"):
                phase_F2(nc, kb, sbt, pst, l, seqs, seq_off, NB, xmid, wgu_bf, wdn_bf, gbc, lnbc, modT, B_modT, ident, epst, B_const, x_dst)
    return nc, kb


def _ln_tile(kb, yt, By, stt, Bst, epst, B_const, g_bc, b_bc, B_bc, outt, Bout, tagn):
    kb.op("dve", lambda e: e.bn_stats(out=stt[:, 0:6], in_=yt[:, 0:512]), reads=[By], writes=[Bst])
    kb.op("dve", lambda e: e.bn_stats(out=stt[:, 6:12], in_=yt[:, 512:1024]), reads=[By], writes=[Bst])
    kb.op("dve", lambda e: e.bn_aggr(out=stt[:, 12:14], in_=stt[:, 0:12]), reads=[Bst], writes=[Bst])
    kb.op("act", lambda e: e.activation(out=stt[:, 14:15], in_=stt[:, 13:14], func=AF.Sqrt, bias=epst[:, 0:1], scale=1.0),
          reads=[Bst, B_const], writes=[Bst])
    kb.op("dve", lambda e: e.reciprocal(out=stt[:, 15:16], in_=stt[:, 14:15]), reads=[Bst], writes=[Bst])
    kb.op("dve", lambda e: e.scalar_tensor_tensor(out=stt[:, 16:17], in0=stt[:, 12:13], scalar=-1.0, in1=stt[:, 15:16], op0=ALU.mult, op1=ALU.mult),
          reads=[Bst], writes=[Bst])
    kb.op("act", lambda e: e.activation(out=yt[:, :], in_=yt[:, :], func=AF.Identity, scale=stt[:, 15:16], bias=stt[:, 16:17]),
          reads=[Bst, By], writes=[By])
    kb.op("pool", lambda e: e.tensor_tensor(out=yt[:, :], in0=yt[:, :], in1=g_bc[:, :], op=ALU.mult), reads=[By, B_bc], writes=[By])
    kb.op("pool", lambda e: e.tensor_tensor(out=outt[:, :], in0=yt[:, :], in1=b_bc[:, :], op=ALU.add), reads=[By, B_bc], writes=[Bout])


def phase_P(nc, kb, sbt, pst, l, seqs, seq_off, NB, x_src, win_bf, rope_in, modT, B_modT, ident, B_const, qkT, vaug, permt):
    with ExitStack() as st:
        w = sbt(st, "P_w", [128, 8, WIN], BF16); Bw = Buf()
        for kc in range(8):
            kb.dma("sp", w[:, kc, :], win_bf.ap()[l, :, kc, :], key=Bw, writes=[Bw])
        xt = [sbt(st, "P_x%d" % i, [128, D]) for i in range(3)]; Bx = [Buf() for _ in range(3)]
        hT = [sbt(st, "P_hT%d" % i, [128, 8, 512], BF16) for i in range(2)]; BhT = [[Buf(), Buf()] for _ in range(2)]
        tab = [sbt(st, "P_tab%d" % i, [128, 4, 512]) for i in range(2)]; Btab = [Buf() for _ in range(2)]
        pT = [pst(st, "P_pT%d" % i, [128, 1024]) for i in range(1)]; BpT = [Buf(), Buf()]
        pc = [pst(st, "P_pc%d" % i, [128, 512]) for i in range(4)]; Bpc = [Buf() for _ in range(4)]
        pv = pst(st, "P_pv", [128, 1024]); Bpv = Buf(); Bpvh = [Buf(), Buf()]
        ob = [sbt(st, "P_ob%d" % i, [128, 512], BF16) for i in range(4)]; Bob = [Buf() for _ in range(4)]
        m1 = [sbt(st, "P_m1%d" % i, [128, 512]) for i in range(2)]; Bm1 = [Buf() for _ in range(2)]
        m2 = [sbt(st, "P_m2%d" % i, [128, 512]) for i in range(2)]; Bm2 = [Buf() for _ in range(2)]
        asb = [sbt(st, "P_asb%d" % i, [128, 512]) for i in range(2)]; Basb = [Buf() for _ in range(2)]
        va = [sbt(st, "P_va%d" % i, [128, 16, 128], BF16) for i in range(2)]; Bva = [Buf() for _ in range(2)]
        for i in range(2):
            kb.op("pool", lambda e, i=i: e.memset(va[i][:], 1.0), writes=[Bva[i]])
        blocks = [(b, t0) for b in range(NB) for t0 in range(0, seqs[b], 512)]
        cnt = {"xi": 0, "oi": 0, "mi": 0, "vi": 0}

        def prep_tile(bi, tt):
            b, t0 = blocks[bi]
            g0 = seq_off[b] + t0
            hb = bi % 2
            if tt == 0:
                kb.dma("sp", tab[hb][:], rope_in.ap()[:, :, t0:t0 + 512].rearrange("k p t -> p k t"), key=Btab[hb], writes=[Btab[hb]])
            xb = cnt["xi"] % 3; cnt["xi"] += 1
            kb.dma("sp", xt[xb][:], x_src.ap()[g0 + tt * 128:g0 + (tt + 1) * 128, :], key=Bx[xb], writes=[Bx[xb]])

            def tr(e, xb=xb):
                for kc in range(8):
                    m = e.transpose(pT[0][:, kc * 128:(kc + 1) * 128], xt[xb][:, kc * 128:(kc + 1) * 128], ident[:])
                return m
            kb.op("pe", tr, reads=[Bx[xb], B_const], writes=[BpT[0], BpT[1]])
            for kc in range(8):
                if kc < 4:
                    kb.op("act", lambda e, kc=kc: e.activation(
                        out=hT[hb][:, kc, tt * 128:(tt + 1) * 128], in_=pT[0][:, kc * 128:(kc + 1) * 128], func=AF.Identity,
                        scale=modT[:, l, 1, kc, b:b + 1], bias=modT[:, l, 0, kc, b:b + 1]),
                        reads=[B_modT], writes=[BpT[0], BhT[hb][0]], own=False)
                else:
                    kb.op("dve", lambda e, kc=kc: e.tensor_scalar(
                        out=hT[hb][:, kc, tt * 128:(tt + 1) * 128], in0=pT[0][:, kc * 128:(kc + 1) * 128],
                        scalar1=modT[:, l, 1, kc, b:b + 1], scalar2=modT[:, l, 0, kc, b:b + 1], op0=ALU.mult, op1=ALU.add),
                        reads=[B_modT], writes=[BpT[1], BhT[hb][1]], own=False)

        for tt in range(4):
            prep_tile(0, tt)
        for bi, (b, t0) in enumerate(blocks):
            g0 = seq_off[b] + t0
            hb = bi % 2
            ngrp = 0

            def hook():
                if bi + 1 < len(blocks) and ngrp in (3, 7, 11, 15):
                    prep_tile(bi + 1, (ngrp - 3) // 4)

            def fm(e, pcx, col0, hb=hb):
                for kc in range(8):
                    m = e.matmul(pcx[:, :], lhsT=w[:, kc, col0:col0 + 128], rhs=hT[hb][:, kc, :], start=(kc == 0), stop=(kc == 7))
                return m
            for ch in range(4):
                pi = ch % 4; o = cnt["oi"] % 4; cnt["oi"] += 1
                kb.op("pe", lambda e, pi=pi, ch=ch, fm=fm: fm(e, pc[pi], ch * 128), reads=[Bw] + BhT[hb], writes=[Bpc[pi]])
                kb.op("act", lambda e, pi=pi, o=o: e.copy(out=ob[o][:, :], in_=pc[pi][:, :]), reads=[], writes=[Bpc[pi], Bob[o]])
                kb.dma("pool", qkT.ap()[ch, :, g0:g0 + 512], ob[o][:, :], key=Bob[o], reads=[Bob[o]])
                hook(); ngrp += 1
            for ch in range(12):
                pa = (2 * ch) % 4; pb = (2 * ch + 1) % 4; o = cnt["oi"] % 4; cnt["oi"] += 1; mm = cnt["mi"] % 2; cnt["mi"] += 1
                ti = 0 if ch < 4 else 2
                pt_ = 0 if ch < 4 else 1
                kb.op("pe", lambda e, pa=pa, ch=ch, fm=fm: fm(e, pc[pa], 512 + ch * 128), reads=[Bw] + BhT[hb], writes=[Bpc[pa]])
                kb.op("act", lambda e, pa=pa, mm=mm: e.copy(out=asb[mm][:, :], in_=pc[pa][:, :]), reads=[], writes=[Bpc[pa], Basb[mm]])
                kb.op("pe", lambda e, pb=pb, mm=mm, pt_=pt_: e.matmul(pc[pb][:, :], lhsT=permt[:, pt_, :], rhs=asb[mm][:, :], start=True, stop=True),
                      reads=[Basb[mm], B_const], writes=[Bpc[pb]])
                kb.op("dve", lambda e, mm=mm, ti=ti, hb=hb: e.tensor_tensor(out=m1[mm][:, :], in0=asb[mm][:, :], in1=tab[hb][:, ti, :], op=ALU.mult),
                      reads=[Btab[hb], Basb[mm]], writes=[Bm1[mm]])
                kb.op("dve", lambda e, pb=pb, mm=mm, ti=ti, hb=hb: e.tensor_tensor(out=m2[mm][:, :], in0=pc[pb][:, :], in1=tab[hb][:, ti + 1, :], op=ALU.mult),
                      reads=[Btab[hb]], writes=[Bpc[pb], Bm2[mm]])
                kb.op("pool", lambda e, mm=mm, o=o: e.tensor_tensor(out=ob[o][:, :], in0=m1[mm][:, :], in1=m2[mm][:, :], op=ALU.add),
                      reads=[Bm1[mm], Bm2[mm]], writes=[Bob[o]])
                kb.dma("pool", qkT.ap()[4 + ch, :, g0:g0 + 512], ob[o][:, :], key=Bob[o], reads=[Bob[o]])
                hook(); ngrp += 1
            for tt in range(4):
                vb_ = cnt["vi"] % 2; cnt["vi"] += 1

                def vm(e, tt=tt, hb=hb):
                    for hf in range(2):
                        for kc in range(8):
                            m = e.matmul(pv[:, hf * 512:(hf + 1) * 512], lhsT=hT[hb][:, kc, tt * 128:(tt + 1) * 128],
                                         rhs=w[:, kc, 2048 + hf * 512:2048 + (hf + 1) * 512], start=(kc == 0), stop=(kc == 7))
                    return m
                kb.op("pe", vm, reads=[Bw] + BhT[hb], writes=[Bpvh[0], Bpvh[1]])
                pvv = pv[:, :].rearrange("p (h two d) -> p h two d", two=2, d=64)
                for two in range(2):
                    cs = slice(0, 64) if two == 0 else slice(64, 128)
                    kb.op("act", lambda e, vb_=vb_, pvv=pvv, two=two, cs=cs: e.copy(out=va[vb_][:, two:8:2, cs], in_=pvv[:, 0:4, two, :]),
                          reads=[], writes=[Bpvh[0], Bva[vb_]], own=False)
                    kb.op("dve", lambda e, vb_=vb_, pvv=pvv, two=two, cs=cs: e.tensor_copy(out=va[vb_][:, 8 + two:16:2, cs], in_=pvv[:, 4:8, two, :]),
                          reads=[], writes=[Bpvh[1], Bva[vb_]], own=False)
                kb.dma("pool", vaug.ap()[g0 + tt * 128:g0 + (tt + 1) * 128, :, :], va[vb_][:, :, :], key=Bva[vb_], reads=[Bva[vb_]])
        kb.flush()


def _pipeline(units, lags=None):
    n = len(units)
    if n == 0:
        return
    ns = len(units[0])
    if lags is None:
        lags = list(range(ns))
    for t in range(n + max(lags)):
        for sidx in range(ns):
            u = t - lags[sidx]
            if 0 <= u < n:
                units[u][sidx]()


def phase_A1(nc, kb, sbt, pst, l, seqs, seq_off, qkT, vaug, mixT, zz_d):
    SM = max(seqs)
    with ExitStack() as st:
        ZZ = sbt(st, "A1_zz", [128, 4, 14, 64], BF16); B_ZZ = Buf()
        kb.dma("sp", ZZ[:].rearrange("p h d c -> p (h d c)"), zz_d.ap()[l, :, :], key=B_ZZ, writes=[B_ZZ])
        ZZv = ZZ[:, :, :, :].rearrange("p (c e) d q -> p e c d q", e=2)
        qT = sbt(st, "A1_q", [128, 2, SM], BF16); kT = sbt(st, "A1_k", [128, 2, SM], BF16); Bq = Buf(); Bk = Buf()
        vt = sbt(st, "A1_v", [128, SM // 128, 4, 128], BF16); Bv = Buf()
        ps = [pst(st, "A1_ps%d" % i, [128, 1024]) for i in range(2)]; Bps = [Buf() for _ in range(2)]
        pa = [pst(st, "A1_pa%d" % i, [128, 4, 64]) for i in range(2)]; Bpa = [Buf() for _ in range(2)]
        ex = [sbt(st, "A1_ex%d" % i, [128, 2, 8, 64], BF16) for i in range(2)]; Bex = [Buf() for _ in range(2)]
        pt = [sbt(st, "A1_pt%d" % i, [128, 2, 8, 64], BF16) for i in range(2)]; Bpt = [Buf() for _ in range(2)]
        rr = [sbt(st, "A1_r%d" % i, [128, 2, 64]) for i in range(2)]; Brr = [Buf() for _ in range(2)]
        mo = [sbt(st, "A1_mo%d" % i, [128, 2, 64], BF16) for i in range(2)]; Bmo = [Buf() for _ in range(2)]
        it = 0
        for b in range(len(seqs)):
            S = seqs[b]; g0 = seq_off[b]; R = S // 64
            for c in range(2):
                kb.dma("sp", qT[:, c, 0:S], qkT.ap()[c, :, g0:g0 + S], key=Bq, writes=[Bq])
                kb.dma("sp", kT[:, c, 0:S], qkT.ap()[2 + c, :, g0:g0 + S], key=Bk, writes=[Bk])
            for par in range(2):
                ntile = S // 128 - par
                kb.dma("sp", vt[:, 0:ntile, :, :], vaug.ap()[g0 + 64 * par:g0 + 64 * par + ntile * 128, 0:4, :].rearrange("(n p) h d -> p n h d", p=128),
                       key=Bv, writes=[Bv])
                units = []
                for r in range(R):
                    rs = min(max(r - 4, 0), R - 8)
                    if rs % 2 != par:
                        continue
                    dl = r - rs
                    i = it % 2; it += 1

                    def s_qk(i=i, r=r, rs=rs):
                        def qk(e):
                            for j in range(4):
                                for c in range(2):
                                    for ee in range(2):
                                        col = ee * 512 + (c * 4 + j) * 64
                                        m = e.matmul(ps[i][:, col:col + 64],
                                                     lhsT=kT[64 * ee:64 * ee + 64, c, rs * 64 + 128 * j:rs * 64 + 128 * (j + 1)],
                                                     rhs=qT[64 * ee:64 * ee + 64, c, r * 64:(r + 1) * 64], start=True, stop=True)
                            return m
                        kb.op("pe", qk, reads=[Bq, Bk], writes=[Bps[i]])

                    def s_mid(i=i, dl=dl):
                        kb.op("act", lambda e: e.activation(out=ex[i][:].rearrange("p a b c -> p (a b c)"), in_=ps[i][:, :], func=AF.Exp, scale=0.125),
                              reads=[], writes=[Bps[i], Bex[i]])
                        for ee in range(2):
                            kb.op(MASK_ENG if ee == 1 else "dve", lambda e, ee=ee: e.tensor_tensor(
                                out=pt[i][:, ee, :, :].rearrange("p (c j) q -> p c j q", c=2), in0=ex[i][:, ee, :, :].rearrange("p (c j) q -> p c j q", c=2),
                                in1=ZZv[:, ee, :, 7 - dl:14 - dl:2, :], op=ALU.mult), reads=[Bex[i], B_ZZ], writes=[Bpt[i]], own=False)

                    def s_pv(i=i, rs=rs, par=par):
                        def pvm(e):
                            tb = (rs - par) // 2
                            for h in range(4):
                                c, ee = h // 2, h % 2
                                for j in range(4):
                                    m = e.matmul(pa[i][:, h, :], lhsT=vt[:, tb + j, h, :], rhs=pt[i][:, ee, c * 4 + j, :], start=(j == 0), stop=(j == 3))
                            return m
                        kb.op("pe", pvm, reads=[Bv, Bpt[i]], writes=[Bpa[i]])

                    def s_post(i=i, r=r, g0=g0):
                        pav = pa[i][:, :, :].rearrange("p (c e) q -> p c e q", e=2)
                        kb.op("act", lambda e: e.activation(out=rr[i][0:64, :, :], in_=pav[64:128, :, 0, :], func=AF.Ln), reads=[], writes=[Bpa[i], Brr[i]])
                        kb.op("act", lambda e: e.activation(out=rr[i][64:128, :, :], in_=pav[0:64, :, 1, :], func=AF.Ln), reads=[], writes=[Bpa[i], Brr[i]], own=False)
                        kb.op("act", lambda e: e.activation(out=rr[i][:, :, :], in_=rr[i][:, :, :], func=AF.Exp, scale=-1.0), reads=[], writes=[Brr[i]])
                        kb.op("dve", lambda e: e.tensor_tensor(out=mo[i][0:64, :, :], in0=pav[0:64, :, 0, :], in1=rr[i][0:64, :, :], op=ALU.mult),
                              reads=[Brr[i]], writes=[Bpa[i], Bmo[i]])
                        kb.op("dve", lambda e: e.tensor_tensor(out=mo[i][64:128, :, :], in0=pav[64:128, :, 1, :], in1=rr[i][64:128, :, :], op=ALU.mult),
                              reads=[Brr[i]], writes=[Bpa[i], Bmo[i]])
                        kb.dma("pool", mixT.ap()[0:2, :, g0 + r * 64:g0 + (r + 1) * 64].rearrange("c p q -> p c q"), mo[i][:, :, :], key=Bmo[i], reads=[Bmo[i]])
                    units.append((s_qk, s_mid, s_pv, s_post))
                _pipeline(units)
        kb.flush()


def phase_A2(nc, kb, sbt, pst, l, seqs, seq_off, qkT, vaug, mixT, neglam, gsub, epst, bdones, B_small, B_const):
    SM = max(seqs)
    sc = 32.0 ** -0.5
    with ExitStack() as st:
        qT = sbt(st, "A2_q", [128, 2, SM], BF16); kT = sbt(st, "A2_k", [128, 2, SM], BF16); Bq = Buf(); Bk = Buf()
        vt = sbt(st, "A2_v", [128, SM // 128, 4, 128], BF16); Bv = Buf()
        ps = [pst(st, "A2_ps%d" % i, [128, 1024]) for i in range(2)]; Bps = [Buf() for _ in range(2)]
        qpad = [sbt(st, "A2_qp%d" % i, [128, 4, 512], BF16) for i in range(2)]; Bqp = [Buf() for _ in range(2)]
        for i in range(2):
            kb.op("pool", lambda e, i=i: e.memset(qpad[i][:], 0.0), writes=[Bqp[i]])
        accs = [[pst(st, "A2_acc%d%d" % (a_, j), [128, 512]) for j in range(2)] for a_ in range(2)]; Baccs = [Buf(), Buf()]
        ai = 0
        pT = [sbt(st, "A2_pT%d" % i, [128, 2, 512], BF16) for i in range(3)]; BpT = [Buf() for _ in range(3)]
        r0 = sbt(st, "A2_r0", [128, 512]); r1 = sbt(st, "A2_r1", [128, 512]); Br = Buf()
        O = [sbt(st, "A2_O%d" % i, [128, 512]) for i in range(2)]; BO = [Buf() for _ in range(2)]
        sq = sbt(st, "A2_sq", [128, 512]); Bsq = Buf()
        mo = [sbt(st, "A2_mo%d" % i, [128, 512], BF16) for i in range(2)]; Bmo = [Buf() for _ in range(2)]
        it = 0; oi = 0
        for b in range(len(seqs)):
            S = seqs[b]; g0 = seq_off[b]; NT = S // 128
            for c in range(2):
                kb.dma("sp", qT[:, c, 0:S], qkT.ap()[4 + c, :, g0:g0 + S], key=Bq, writes=[Bq])
                kb.dma("sp", kT[:, c, 0:S], qkT.ap()[6 + c, :, g0:g0 + S], key=Bk, writes=[Bk])
            kb.dma("sp", vt[:, 0:NT, :, :], vaug.ap()[g0:g0 + S, 4:8, :].rearrange("(n p) h d -> p n h d", p=128), key=Bv, writes=[Bv])
            units = []
            for g in range(2):
                for qc in range(S // 512):
                    o = oi % 2; oi += 1
                    nb = o
                    for ee in range(2):
                        h = 2 * g + ee
                        acc = accs[ai % 2]; Bacc = Baccs[ai % 2]; ai += 1
                        pss = acc[0]; Bss = Bacc
                        for kt in range(NT):
                            i = it % 2; i3 = it % 3; it += 1

                            def s_qk(i=i, kt=kt, ee=ee, g=g, nb=nb, qc=qc):
                                if kt == 0 and ee == 0:
                                    for cidx in range(4):
                                        kb.op("pool", lambda e, cidx=cidx: e.tensor_copy(
                                            out=qpad[nb][32 * cidx:32 * cidx + 32, cidx, :], in_=qT[32 * cidx:32 * cidx + 32, g, qc * 512:(qc + 1) * 512]),
                                            reads=[Bq], writes=[Bqp[nb]])

                                def qk(e):
                                    for mp in range(2):
                                        m = e.matmul(ps[i][:, mp * 512:(mp + 1) * 512], lhsT=kT[:, g, kt * 128:(kt + 1) * 128],
                                                     rhs=qpad[nb][:, 2 * ee + mp, :], start=True, stop=True)
                                    return m
                                kb.op("pe", qk, reads=[Bk, Bqp[nb]], writes=[Bps[i]])

                            def s_mid(i=i, i3=i3):
                                kb.op("act", lambda e: e.activation(out=pT[i3][:].rearrange("p a b -> p (a b)"), in_=ps[i][:, :], func=AF.Exp, scale=sc),
                                      reads=[], writes=[Bps[i], BpT[i3]])

                            def s_pv(i3=i3, kt=kt, h=h, NT=NT, acc=acc, Bacc=Bacc):
                                def pv(e):
                                    for mp in range(2):
                                        m = e.matmul(acc[mp][:, :], lhsT=vt[:, kt, h, :], rhs=pT[i3][:, mp, :], start=(kt == 0), stop=(kt == NT - 1))
                                    return m
                                kb.op("pe", pv, reads=[Bv, BpT[i3]], writes=[Bacc])

                            def s_post(kt=kt, NT=NT, ee=ee, o=o, g=g, qc=qc, g0=g0, acc=acc, Bacc=Bacc, pss=pss, Bss=Bss):
                                if kt != NT - 1:
                                    return
                                import os
                                lvl = int(os.environ.get("A2DBG", "9"))
                                if lvl < 1:
                                    return
                                npr = slice(0, 64) if ee == 0 else slice(64, 128)
                                dpr = slice(64, 128) if ee == 0 else slice(0, 64)
                                kb.op("dve", lambda e: e.reciprocal(out=r0[npr, :], in_=acc[0][dpr, :]), reads=[], writes=[Bacc, Br])
                                kb.op("dve", lambda e: e.reciprocal(out=r1[npr, :], in_=acc[1][dpr, :]), reads=[], writes=[Bacc, Br])
                                kb.op("dve", lambda e: e.tensor_tensor(out=r0[npr, :], in0=acc[0][npr, :], in1=r0[npr, :], op=ALU.mult), reads=[Br], writes=[Bacc, Br])
                                kb.op("dve", lambda e: e.tensor_tensor(out=r1[npr, :], in0=acc[1][npr, :], in1=r1[npr, :], op=ALU.mult), reads=[Br], writes=[Bacc, Br])
                                kb.op("dve", lambda e: e.scalar_tensor_tensor(out=O[o][npr, :], in0=r1[npr, :], scalar=neglam[npr, l:l + 1], in1=r0[npr, :],
                                                                             op0=ALU.mult, op1=ALU.add), reads=[Br, B_small], writes=[BO[o]])
                                if ee == 0 or lvl < 2:
                                    return
                                kb.op("pool", lambda e: e.tensor_tensor(out=sq[:, :], in0=O[o][:, :], in1=O[o][:, :], op=ALU.mult), reads=[BO[o]], writes=[Bsq])
                                kb.op("pe", lambda e: e.matmul(pss[:, :], lhsT=bdones[:, :], rhs=sq[:, :], start=True, stop=True), reads=[Bsq, B_const], writes=[Bss])
                                if lvl < 3:
                                    return
                                kb.op("act", lambda e: e.activation(out=sq[:, :], in_=pss[:, :], func=AF.Ln, bias=epst[:, 0:1], scale=1.0 / 64.0),
                                      reads=[B_const], writes=[Bss, Bsq])
                                kb.op("act", lambda e: e.activation(out=sq[:, :], in_=sq[:, :], func=AF.Exp, scale=-0.5), reads=[], writes=[Bsq])
                                kb.op("dve", lambda e: e.scalar_tensor_tensor(out=mo[o][:, :], in0=O[o][:, :], scalar=gsub[:, l:l + 1], in1=sq[:, :],
                                                                             op0=ALU.mult, op1=ALU.mult), reads=[BO[o], Bsq, B_small], writes=[Bmo[o]])
                                kb.dma("pool", mixT.ap()[2 + g, :, g0 + qc * 512:g0 + (qc + 1) * 512], mo[o][:, :], key=Bmo[o], reads=[Bmo[o]])
                            units.append((s_qk, s_mid, s_pv, s_post))
            _pipeline(units)
        kb.flush()


def phase_A3(nc, kb, sbt, pst, l, seqs, seq_off, qkT, vaug, mixT, bandm, B_const):
    SM = max(seqs)
    NBUF = 4
    with ExitStack() as st:
        qT = sbt(st, "A3_q", [128, SM], BF16); kT = sbt(st, "A3_k", [128, SM], BF16); Bq = Buf(); Bk = Buf()
        vts = [sbt(st, "A3_v%d" % i, [128, SM // 128, 2, 128], BF16) for i in range(2)]; Bvs = [Buf(), Buf()]
        ACC = sbt(st, "A3_acc", [128, 2, SM]); BACC = Buf()
        ps = [pst(st, "A3_ps%d" % i, [128, 512]) for i in range(NBUF)]; Bps = [Buf() for _ in range(NBUF)]
        pa = [pst(st, "A3_pa%d" % i, [128, 512]) for i in range(NBUF)]; Bpa = [Buf() for _ in range(NBUF)]
        ex = [sbt(st, "A3_ex%d" % i, [128, 3, 128], BF16) for i in range(NBUF)]; Bex = [Buf() for _ in range(NBUF)]
        pt = [sbt(st, "A3_pt%d" % i, [128, 3, 128], BF16) for i in range(NBUF)]; Bpt = [Buf() for _ in range(NBUF)]
        rr = [sbt(st, "A3_r%d" % i, [128, 1024]) for i in range(2)]; Brr = [Buf() for _ in range(2)]
        mo = [sbt(st, "A3_mo%d" % i, [128, 1024], BF16) for i in range(2)]; Bmo = [Buf() for _ in range(2)]
        groups = [(b, c, pi, d) for b in range(len(seqs)) for c in range(4) for pi, d in enumerate((1, 4, 16))]

        def load_v(G):
            b, c, pi, d = groups[G]
            S = seqs[b]; g0 = seq_off[b]; nm = S // (128 * d)
            vt = vts[G % 2]; Bv = Bvs[G % 2]
            for rho in range(d):
                src = bass.AP(vaug, (g0 + rho) * 2048 + (8 + 2 * c) * 128, [[d * 2048, 128], [128 * d * 2048, nm], [1, 256]])
                kb.dma("sp", vt[:, rho * nm:(rho + 1) * nm, :, :].rearrange("p n h d -> p n (h d)"), src, key=Bv, writes=[Bv])

        def load_qk(b, c):
            S = seqs[b]; g0 = seq_off[b]
            kb.dma("sp", qT[:, 0:S], qkT.ap()[8 + c, :, g0:g0 + S], key=Bq, writes=[Bq])
            kb.dma("sp", kT[:, 0:S], qkT.ap()[12 + c, :, g0:g0 + S], key=Bk, writes=[Bk])

        def finish(b, c, oi0):
            S = seqs[b]; g0 = seq_off[b]
            for n_, p0 in enumerate(range(0, S, 1024)):
                o = (oi0 + n_) % 2
                sl = slice(p0, p0 + 1024)
                kb.op("act", lambda e, o=o, sl=sl: e.activation(out=rr[o][0:64, :], in_=ACC[64:128, 0, sl], func=AF.Ln), reads=[BACC], writes=[Brr[o]])
                kb.op("act", lambda e, o=o, sl=sl: e.activation(out=rr[o][64:128, :], in_=ACC[0:64, 1, sl], func=AF.Ln), reads=[BACC], writes=[Brr[o]], own=False)
                kb.op("act", lambda e, o=o: e.activation(out=rr[o][:, :], in_=rr[o][:, :], func=AF.Exp, scale=-1.0), reads=[], writes=[Brr[o]])
                kb.op("pool", lambda e, o=o, sl=sl: e.tensor_tensor(out=mo[o][0:64, :], in0=ACC[0:64, 0, sl], in1=rr[o][0:64, :], op=ALU.mult),
                      reads=[BACC, Brr[o]], writes=[Bmo[o]])
                kb.op("pool", lambda e, o=o, sl=sl: e.tensor_tensor(out=mo[o][64:128, :], in0=ACC[64:128, 1, sl], in1=rr[o][64:128, :], op=ALU.mult),
                      reads=[BACC, Brr[o]], writes=[Bmo[o]])
                kb.dma("pool", mixT.ap()[4 + c, :, g0 + p0:g0 + p0 + 1024], mo[o][:, :], key=Bmo[o], reads=[Bmo[o]])

        units = []
        it = 0
        for G, (b, c, pi, d) in enumerate(groups):
            S = seqs[b]; nm = S // (128 * d)
            vt = vts[G % 2]; Bv = Bvs[G % 2]
            nun = d * nm * 2
            un = 0
            for rho in range(d):
                for m in range(nm):
                    olo = -1 if m > 0 else 0
                    ohi = 1 if m < nm - 1 else 0
                    qsl = slice(rho + d * 128 * m, rho + d * 128 * m + d * 127 + 1, d)
                    a, z = olo + 1, ohi + 2
                    for ee in range(2):
                        i = it % NBUF; it += 1
                        is_first = (un == 0)
                        un += 1
                        is_last = (un == nun)

                        def s_qk(i=i, m=m, olo=olo, ohi=ohi, qsl=qsl, rho=rho, d=d, is_first=is_first, b=b, c=c, pi=pi, ee=ee):
                            if is_first and pi == 0:
                                load_qk(b, c)

                            def qk(e):
                                for o in range(olo, ohi + 1):
                                    ksl = slice(rho + d * 128 * (m + o), rho + d * 128 * (m + o) + d * 127 + 1, d)
                                    mm = e.matmul(ps[i][:, (o + 1) * 128:(o + 2) * 128], lhsT=kT[64 * ee:64 * ee + 64, ksl],
                                                  rhs=qT[64 * ee:64 * ee + 64, qsl], start=True, stop=True)
                                return mm
                            kb.op("pe", qk, reads=[Bq, Bk], writes=[Bps[i]])

                        def s_exp(i=i, a=a, z=z):
                            kb.op("act", lambda e: e.activation(out=ex[i][:, a:z, :], in_=ps[i][:, a * 128:z * 128].rearrange("p (o q) -> p o q", q=128),
                                                                func=AF.Exp, scale=0.125), reads=[], writes=[Bps[i], Bex[i]])

                        def s_mask(i=i, a=a, z=z):
                            kb.op("dve", lambda e: e.tensor_tensor(out=pt[i][:, a:z, :], in0=ex[i][:, a:z, :], in1=bandm[:, a:z, :], op=ALU.mult),
                                  reads=[Bex[i], B_const], writes=[Bpt[i]])

                        def s_pv(i=i, m=m, olo=olo, ohi=ohi, rho=rho, nm=nm, vt=vt, Bv=Bv, ee=ee):
                            def pvm(e):
                                for o in range(olo, ohi + 1):
                                    mm = e.matmul(pa[i][:, 0:128], lhsT=vt[:, rho * nm + m + o, ee, :], rhs=pt[i][:, o + 1, :],
                                                  start=(o == olo), stop=(o == ohi))
                                return mm
                            kb.op("pe", pvm, reads=[Bv, Bpt[i]], writes=[Bpa[i]])

                        def s_post(i=i, qsl=qsl, pi=pi, is_last=is_last, b=b, c=c, G=G, is_first=is_first, ee=ee):
                            if is_first and G + 1 < len(groups):
                                load_v(G + 1)
                            if pi == 0:
                                kb.op("dve", lambda e: e.tensor_copy(out=ACC[:, ee, qsl], in_=pa[i][:, 0:128]), reads=[], writes=[Bpa[i], BACC])
                            else:
                                kb.op("dve", lambda e: e.tensor_tensor(out=ACC[:, ee, qsl], in0=pa[i][:, 0:128], in1=ACC[:, ee, qsl], op=ALU.add),
                                      reads=[], writes=[Bpa[i], BACC])
                            if is_last and pi == 2:
                                finish(b, c, G)
                        units.append((s_qk, s_exp, s_mask, s_pv, s_post))
        load_v(0)
        _pipeline(units)
        kb.flush()


def phase_F1(nc, kb, sbt, pst, l, seqs, seq_off, NB, x_src, mixT, wout_bf, gbc, lnbc, epst, B_const, xmid):
    with ExitStack() as st:
        w = sbt(st, "F1_w", [128, 8, D], BF16); Bw = Buf()
        kb.dma("sp", w[:], wout_bf.ap()[l, :, :, :], key=Bw, writes=[Bw])
        lg = sbt(st, "F1_lg", [128, D]); lb = sbt(st, "F1_lb", [128, D]); Bl = Buf()
        kb.dma("sp", lg[:], lnbc.ap()[l, 0, :, :], key=Bl, writes=[Bl])
        kb.dma("sp", lb[:], lnbc.ap()[l, 1, :, :], key=Bl, writes=[Bl])
        g1 = sbt(st, "F1_g1", [128, D]); Bg = Buf()
        mx = [sbt(st, "F1_mx%d" % i, [128, 8, 512], BF16) for i in range(2)]; Bmx = [Buf() for _ in range(2)]
        xt = [sbt(st, "F1_x%d" % i, [128, D]) for i in range(3)]; Bx = [Buf() for _ in range(3)]
        yt = [sbt(st, "F1_y%d" % i, [128, D]) for i in range(3)]; By = [Buf() for _ in range(3)]
        ot = [sbt(st, "F1_o%d" % i, [128, D]) for i in range(2)]; Bo = [Buf() for _ in range(2)]
        stt = [sbt(st, "F1_st%d" % i, [128, 32]) for i in range(2)]; Bst = [Buf() for _ in range(2)]
        po = [pst(st, "F1_po%d" % i, [128, 1024]) for i in range(2)]; Bpo = [Buf() for _ in range(2)]
        blk = 0; ti = 0
        for b in range(NB):
            kb.dma("sp", g1[:], gbc.ap()[l, b, 0, :, :], key=Bg, writes=[Bg])
            for t0 in range(0, seqs[b], 512):
                g0 = seq_off[b] + t0
                mb = blk % 2; blk += 1
                kb.dma("sp", mx[mb][:], mixT.ap()[:, :, g0:g0 + 512].rearrange("c p t -> p c t"), key=Bmx[mb], writes=[Bmx[mb]])
                for tt in range(4):
                    i3 = ti % 3; i2 = ti % 2; ti += 1
                    r0_ = g0 + tt * 128
                    kb.dma("sp", xt[i3][:], x_src.ap()[r0_:r0_ + 128, :], key=Bx[i3], writes=[Bx[i3]])

                    def mm(e, i2=i2, mb=mb, tt=tt):
                        for hf in range(2):
                            for fc in range(8):
                                m = e.matmul(po[i2][:, hf * 512:(hf + 1) * 512], lhsT=mx[mb][:, fc, tt * 128:(tt + 1) * 128],
                                             rhs=w[:, fc, hf * 512:(hf + 1) * 512], start=(fc == 0), stop=(fc == 7))
                        return m
                    kb.op("pe", mm, reads=[Bw, Bmx[mb]], writes=[Bpo[i2]])
                    kb.op("dve", lambda e, i3=i3, i2=i2: e.tensor_tensor(out=yt[i3][:, :], in0=po[i2][:, :], in1=g1[:, :], op=ALU.mult),
                          reads=[Bg], writes=[Bpo[i2], By[i3]])
                    kb.op("dve", lambda e, i3=i3: e.scalar_tensor_tensor(out=yt[i3][:, :], in0=xt[i3][:, :], scalar=ALPHA, in1=yt[i3][:, :], op0=ALU.mult, op1=ALU.add),
                          reads=[Bx[i3]], writes=[By[i3]])
                    _ln_tile(kb, yt[i3], By[i3], stt[i2], Bst[i2], epst, B_const, lg, lb, Bl, ot[i2], Bo[i2], "F1")
                    kb.dma("pool", xmid.ap()[r0_:r0_ + 128, :], ot[i2][:, :], key=Bo[i2], reads=[Bo[i2]])
        kb.flush()


def phase_F2(nc, kb, sbt, pst, l, seqs, seq_off, NB, xmid, wgu_bf, wdn_bf, gbc, lnbc, modT, B_modT, ident, epst, B_const, x_dst):
    with ExitStack() as st:
        wg = sbt(st, "F2_wg", [128, 8, 2 * NFF], BF16); Bwg = Buf()
        for kc in range(8):
            kb.dma("sp", wg[:, kc, :], wgu_bf.ap()[l, :, kc, :], key=Bwg, writes=[Bwg])
        wd = sbt(st, "F2_wd", [128, NFC, D], BF16); Bwd = Buf()
        kb.dma("sp", wd[:], wdn_bf.ap()[l, :, :, :], key=Bwd, writes=[Bwd])
        lg = sbt(st, "F2_lg", [128, D]); lb = sbt(st, "F2_lb", [128, D]); Bl = Buf()
        kb.dma("sp", lg[:], lnbc.ap()[l, 2, :, :], key=Bl, writes=[Bl])
        kb.dma("sp", lb[:], lnbc.ap()[l, 3, :, :], key=Bl, writes=[Bl])
        g2 = sbt(st, "F2_g2", [128, D]); Bg = Buf()
        xa = [sbt(st, "F2_xa%d" % i, [128, D]) for i in range(2)]; Bxa = [Buf() for _ in range(2)]
        xb = [sbt(st, "F2_xb%d" % i, [128, D]) for i in range(2)]; Bxb = [Buf() for _ in range(2)]
        hTs = [sbt(st, "F2_hT%d" % i, [128, 8, 512], BF16) for i in range(2)]; BhTs = [[Buf(), Buf()] for _ in range(2)]
        aT = sbt(st, "F2_aT", [128, NFC, 512], BF16); BaT = Buf()
        sl = [sbt(st, "F2_sl%d" % i, [128, 512], BF16) for i in range(2)]; Bsl = [Buf() for _ in range(2)]
        stt = [sbt(st, "F2_st%d" % i, [128, 32]) for i in range(2)]; Bst = [Buf() for _ in range(2)]
        pT = pst(st, "F2_pT", [128, 1024]); BpT = [Buf(), Buf()]
        pg = [pst(st, "F2_pg%d" % i, [128, 1024]) for i in range(2)]; Bpg = [Buf() for _ in range(2)]
        po = pst(st, "F2_po", [128, 1024]); Bpo = Buf()
        blocks = [(b, t0) for b in range(NB) for t0 in range(0, seqs[b], 512)]
        cnt = {"xi": 0, "gi": 0, "yi": 0}

        def prep_tile(bi, tt):
            b, t0 = blocks[bi]
            g0 = seq_off[b] + t0
            hT = hTs[bi % 2]; BhT = BhTs[bi % 2]
            i = cnt["xi"] % 2; cnt["xi"] += 1
            kb.dma("sp", xa[i][:], xmid.ap()[g0 + tt * 128:g0 + (tt + 1) * 128, :], key=Bxa[i], writes=[Bxa[i]])

            def tr(e, i=i):
                for kc in range(8):
                    m = e.transpose(pT[:, kc * 128:(kc + 1) * 128], xa[i][:, kc * 128:(kc + 1) * 128], ident[:])
                return m
            kb.op("pe", tr, reads=[Bxa[i], B_const], writes=[BpT[0], BpT[1]])
            for kc in range(8):
                if kc < 4:
                    kb.op("act", lambda e, kc=kc: e.activation(
                        out=hT[:, kc, tt * 128:(tt + 1) * 128], in_=pT[:, kc * 128:(kc + 1) * 128], func=AF.Identity,
                        scale=modT[:, l, 3, kc, b:b + 1], bias=modT[:, l, 2, kc, b:b + 1]), reads=[B_modT], writes=[BpT[0], BhT[0]], own=False)
                else:
                    kb.op("dve", lambda e, kc=kc: e.tensor_scalar(
                        out=hT[:, kc, tt * 128:(tt + 1) * 128], in0=pT[:, kc * 128:(kc + 1) * 128],
                        scalar1=modT[:, l, 3, kc, b:b + 1], scalar2=modT[:, l, 2, kc, b:b + 1], op0=ALU.mult, op1=ALU.add),
                        reads=[B_modT], writes=[BpT[1], BhT[1]], own=False)

        for tt in range(4):
            prep_tile(0, tt)
        cur_b = -1
        for bi, (b, t0) in enumerate(blocks):
            g0 = seq_off[b] + t0
            hT = hTs[bi % 2]; BhT = BhTs[bi % 2]
            if b != cur_b:
                cur_b = b
                kb.dma("sp", g2[:], gbc.ap()[l, b, 1, :, :], key=Bg, writes=[Bg])
            for fc in range(NFC):
                i = cnt["gi"] % 2; cnt["gi"] += 1

                def gu(e, i=i, fc=fc, hT=hT):
                    for part in range(2):
                        col = part * NFF + fc * 128
                        for kc in range(8):
                            m = e.matmul(pg[i][:, part * 512:(part + 1) * 512], lhsT=wg[:, kc, col:col + 128], rhs=hT[:, kc, :],
                                         start=(kc == 0), stop=(kc == 7))
                    return m
                kb.op("pe", gu, reads=[Bwg] + BhT, writes=[Bpg[i]])
                kb.op("act", lambda e, i=i: e.activation(out=sl[i][:, :], in_=pg[i][:, 0:512], func=AF.Silu), reads=[], writes=[Bpg[i], Bsl[i]])
                kb.op("dve", lambda e, i=i, fc=fc: e.tensor_tensor(out=aT[:, fc, :], in0=pg[i][:, 512:1024], in1=sl[i][:, :], op=ALU.mult),
                      reads=[Bsl[i]], writes=[Bpg[i], BaT])
                if bi + 1 < len(blocks) and fc in (4, 9, 14, 19):
                    prep_tile(bi + 1, (fc - 4) // 5)
            for tt in range(4):
                i = cnt["yi"] % 2; cnt["yi"] += 1
                r0_ = g0 + tt * 128
                kb.dma("sp", xb[i][:], xmid.ap()[r0_:r0_ + 128, :], key=Bxb[i], writes=[Bxb[i]])

                def dn(e, tt=tt):
                    for hf in range(2):
                        for fc in range(NFC):
                            m = e.matmul(po[:, hf * 512:(hf + 1) * 512], lhsT=aT[:, fc, tt * 128:(tt + 1) * 128],
                                         rhs=wd[:, fc, hf * 512:(hf + 1) * 512], start=(fc == 0), stop=(fc == NFC - 1))
                    return m
                kb.op("pe", dn, reads=[Bwd, BaT], writes=[Bpo])
                kb.op("dve", lambda e, i=i: e.tensor_scalar(out=xb[i][:, :], in0=xb[i][:, :], scalar1=ALPHA, scalar2=None, op0=ALU.mult),
                      reads=[], writes=[Bxb[i]])
                kb.op("dve", lambda e: e.tensor_tensor(out=po[:, :], in0=po[:, :], in1=g2[:, :], op=ALU.mult), reads=[Bg], writes=[Bpo])
                kb.op("dve", lambda e, i=i: e.tensor_tensor(out=xb[i][:, :], in0=po[:, :], in1=xb[i][:, :], op=ALU.add), reads=[], writes=[Bpo, Bxb[i]])
                _ln_tile(kb, xb[i], Bxb[i], stt[i], Bst[i], epst, B_const, lg, lb, Bl, xb[i], Bxb[i], "F2")
                kb.dma("pool", x_dst.ap()[r0_:r0_ + 128, :], xb[i][:, :], key=Bxb[i], reads=[Bxb[i]])
        kb.flush()


_PERM = _win_perm()


def _core_inputs(xc, cc, shared):
    NB = cc.shape[0]
    cT = np.ascontiguousarray(cc.reshape(NB, 8, 128).transpose(2, 1, 0)).astype(np.float32)
    d = {"x": np.ascontiguousarray(xc), "cT": cT}
    d.update(shared)
    return d


def _shared_inputs(w_ada, b_ada, w_in, na_rpb, diff_lambda, diff_subln_g, w_out, ln1_g, ln1_b, w_gu, w_down, ln2_g, ln2_b, smax):
    L = w_ada.shape[0]
    ident, colmask2, band, bd, perms = _consts()
    f = lambda a: np.ascontiguousarray(np.asarray(a, dtype=np.float32))
    return {
        "w_ada": f(w_ada), "b_ada": f(b_ada), "w_in": f(np.asarray(w_in)[:, :, _PERM]),
        "rpb": f(np.asarray(na_rpb).reshape(L, 60, 31)), "lam": f(np.asarray(diff_lambda).reshape(L, 128)),
        "subg": f(np.tile(np.asarray(diff_subln_g), (1, 2))), "w_out": f(w_out), "w_gu": f(w_gu), "w_down": f(w_down),
        "lnp": f(np.stack([ln1_g, ln1_b, ln2_g, ln2_b], 1)), "rope": _rope_tables(smax),
        "ident": ident, "colmask": colmask2, "band": band, "bd": bd, "perms": perms,
    }


def kernel(x_prompt, x_sample, c_prompt, c_sample, w_ada, b_ada, w_in, na_rpb, diff_lambda, diff_subln_g, w_out,
           ln1_g, ln1_b, w_gu, w_down, ln2_g, ln2_b):
    x_prompt = np.asarray(x_prompt); x_sample = np.asarray(x_sample)
    c_prompt = np.asarray(c_prompt); c_sample = np.asarray(c_sample)
    NCORE = 8
    Bp, Sp, _ = x_prompt.shape; Bs, Ss, _ = x_sample.shape
    per = Bs // NCORE
    seqs = [Sp] + [Ss] * per
    depth = np.asarray(w_ada).shape[0]
    nc, kb = build(seqs, depth)
    shared = _shared_inputs(w_ada, b_ada, w_in, na_rpb, diff_lambda, diff_subln_g, w_out, ln1_g, ln1_b, w_gu, w_down, ln2_g, ln2_b, max(seqs))
    in_maps = []
    for c in range(NCORE):
        xc = np.concatenate([x_prompt[c]] + [x_sample[c * per + j] for j in range(per)], 0)
        cc = np.concatenate([c_prompt[c:c + 1], c_sample[c * per:(c + 1) * per]], 0)
        in_maps.append(_core_inputs(xc, cc, shared))
    res = run_bass_kernel_spmd(nc, in_maps, core_ids=list(range(NCORE)))
    yp = np.empty((Bp, Sp, D), np.float32); ys = np.empty((Bs, Ss, D), np.float32)
    for c in range(NCORE):
        y = res.results[c]["y"]
        yp[c] = y[0:Sp]
        for j in range(per):
            ys[c * per + j] = y[Sp + j * Ss:Sp + (j + 1) * Ss]
    return (yp, ys)
```
